# Optimizing a Trainium2 kernel written in Bass

```python
import jax, jax.numpy as jnp
from jax import lax
import numpy as np

D_MODEL = 1024
BATCH = 16
SEQ = 4096
DEPTH = 1
DEC_BATCH = 16
DEC_SEQ = 32
PAST_LEN = 2048

CHUNK = 64
N_META = 16
D_FF = 2816
D_CONV = 1024
CONV_WIDTH = 3
N_HEADS = 16
QK_NOPE = 64
QK_ROPE = 32
QK_DIM = QK_NOPE + QK_ROPE
V_HEAD = 64
Q_LORA = 384
KV_LORA = 128
ROPE_THETA = 10000.0
RMS_EPS = 1e-6
Q_BLOCK = 128
NEG_INF = -1e30
COL_SIZES = (D_CONV, D_CONV, D_CONV, Q_LORA, KV_LORA, QK_ROPE, D_MODEL, D_MODEL)
D_IN_ALL = sum(COL_SIZES)

kernel_name = "hybrid_shortconv_mla_macaron_stream_step"


def rms_norm(x, g):
    xf = x.astype(jnp.float32)
    y = xf * lax.rsqrt(jnp.mean(xf * xf, axis=-1, keepdims=True) + RMS_EPS)
    return (y * g.astype(jnp.float32)).astype(x.dtype)


def half_step_ffn(x, g, w_gate, w_up, w_down):
    h = rms_norm(x, g)
    return x + 0.5 * ((jax.nn.silu(h @ w_gate) * (h @ w_up)) @ w_down)


def rotary(x, pos):
    half = QK_ROPE // 2
    inv_freq = ROPE_THETA ** (-jnp.arange(half, dtype=jnp.float32) / half)
    ang = pos.astype(jnp.float32)[:, None] * inv_freq[None, :]
    cos = jnp.cos(ang)[None, :, None, :]
    sin = jnp.sin(ang)[None, :, None, :]
    xf = x.astype(jnp.float32)
    x1, x2 = xf[..., :half], xf[..., half:]
    return jnp.concatenate([x1 * cos - x2 * sin, x2 * cos + x1 * sin], axis=-1).astype(x.dtype)


def split_columns(z):
    offsets = [int(o) for o in np.cumsum(COL_SIZES)[:-1]]
    return jnp.split(z, offsets, axis=-1)


def project_inputs(h, pos, w_in_all, q_a_norm, w_uq, kv_a_norm, q_norm):
    bsz, L, _ = h.shape
    b_gate, c_gate, v_conv, q_lat, kv_lat, k_pe, g_conv, g_mla = split_columns(h @ w_in_all)
    conv_in = c_gate * v_conv
    q = (rms_norm(q_lat, q_a_norm) @ w_uq).reshape(bsz, L, N_HEADS, QK_DIM)
    q = jnp.concatenate([q[..., :QK_NOPE], rotary(q[..., QK_NOPE:], pos)], axis=-1)
    q = rms_norm(q, q_norm)
    c_kv = rms_norm(kv_lat, kv_a_norm)
    k_pe = rotary(k_pe[:, :, None, :], pos)[:, :, 0, :]
    return b_gate, conv_in, q, c_kv, k_pe, g_conv, g_mla


def expand_keys(c_kv, k_pe, w_ukv, k_norm):
    bsz, L, _ = c_kv.shape
    kv = (c_kv @ w_ukv).reshape(bsz, L, N_HEADS, QK_NOPE + V_HEAD)
    k_rot = jnp.broadcast_to(k_pe[:, :, None, :], (bsz, L, N_HEADS, QK_ROPE))
    k = rms_norm(jnp.concatenate([kv[..., :QK_NOPE], k_rot], axis=-1), k_norm)
    return k, kv[..., QK_NOPE:]


def attend(q, k, v, mask):
    s = jnp.einsum("bqhd,bkhd->bhqk", q, k).astype(jnp.float32) * (QK_DIM ** -0.5)
    if mask is not None:
        s = jnp.where(mask[None, None], s, NEG_INF)
    p = jax.nn.softmax(s, axis=-1).astype(v.dtype)
    return jnp.einsum("bhqk,bkhd->bqhd", p, v)


def prompt_attention(q, k, v):
    bsz, L = q.shape[0], q.shape[1]
    n_blk = -(-L // Q_BLOCK)
    L_pad = n_blk * Q_BLOCK
    chunk_id = (jnp.arange(L_pad, dtype=jnp.int32) - N_META) // CHUNK
    key_chunk = chunk_id[:L]
    q_blocks = jnp.pad(q, ((0, 0), (0, L_pad - L), (0, 0), (0, 0)))
    q_blocks = q_blocks.reshape(bsz, n_blk, Q_BLOCK, N_HEADS, QK_DIM).transpose(1, 0, 2, 3, 4)

    def one_block(args):
        qb, qc = args
        return attend(qb, k, v, key_chunk[None, :] <= qc[:, None])

    o = lax.map(one_block, (q_blocks, chunk_id.reshape(n_blk, Q_BLOCK)))
    return o.transpose(1, 0, 2, 3, 4).reshape(bsz, L_pad, N_HEADS * V_HEAD)[:, :L]


def depthwise_causal_conv(x_ext, w):
    return lax.conv_general_dilated(
        x_ext, w[:, None, :].astype(x_ext.dtype), window_strides=(1,), padding="VALID",
        dimension_numbers=("NWC", "WIO", "NWC"), feature_group_count=x_ext.shape[-1])


def merge_branches(x, b_gate, conv_y, attn, g_conv, g_mla, w_conv_out, w_mla_out, w_out_all):
    conv_branch = (b_gate * conv_y) @ w_conv_out
    mla_branch = attn @ w_mla_out
    merged = jax.nn.sigmoid(g_conv) * conv_branch + jax.nn.sigmoid(g_mla) * mla_branch
    return x + merged @ w_out_all


def setup_inputs(seed: int = 0) -> dict:
    key = jax.random.key(seed)
    ks = jax.random.split(key, 32)
    f32 = jnp.float32

    def nrm(k, shape, scale):
        return jax.random.normal(k, shape, f32) * scale

    def gain(k, dim):
        return 1.0 + 0.02 * jax.random.normal(k, (DEPTH, dim), f32)

    return {
        "x_prompt": nrm(ks[0], (BATCH, SEQ, D_MODEL), 1.0),
        "x_sample": nrm(ks[1], (DEC_BATCH, DEC_SEQ, D_MODEL), 1.0),
        "cache_conv": nrm(ks[2], (DEPTH, DEC_BATCH, CONV_WIDTH - 1, D_CONV), 1.0),
        "cache_kv_latent": nrm(ks[3], (DEPTH, DEC_BATCH, PAST_LEN, KV_LORA), 1.0),
        "cache_k_rope": nrm(ks[4], (DEPTH, DEC_BATCH, PAST_LEN, QK_ROPE), 1.0),
        "meta_tokens": nrm(ks[5], (N_META, D_MODEL), 1.0),
        "ffn1_norm": gain(ks[6], D_MODEL),
        "ffn1_w_gate": nrm(ks[7], (DEPTH, D_MODEL, D_FF), D_MODEL ** -0.5),
        "ffn1_w_up": nrm(ks[8], (DEPTH, D_MODEL, D_FF), D_MODEL ** -0.5),
        "ffn1_w_down": nrm(ks[9], (DEPTH, D_FF, D_MODEL), D_FF ** -0.5),
        "mix_norm": gain(ks[10], D_MODEL),
        "w_in_all": nrm(ks[11], (DEPTH, D_MODEL, D_IN_ALL), D_MODEL ** -0.5),
        "conv_w": nrm(ks[12], (DEPTH, CONV_WIDTH, D_CONV), CONV_WIDTH ** -0.5),
        "w_conv_out": nrm(ks[13], (DEPTH, D_CONV, D_MODEL), D_CONV ** -0.5),
        "q_a_norm": gain(ks[14], Q_LORA),
        "w_uq": nrm(ks[15], (DEPTH, Q_LORA, N_HEADS * QK_DIM), Q_LORA ** -0.5),
        "kv_a_norm": gain(ks[16], KV_LORA),
        "w_ukv": nrm(ks[17], (DEPTH, KV_LORA, N_HEADS * (QK_NOPE + V_HEAD)), KV_LORA ** -0.5),
        "q_norm": gain(ks[18], QK_DIM),
        "k_norm": gain(ks[19], QK_DIM),
        "w_mla_out": nrm(ks[20], (DEPTH, N_HEADS * V_HEAD, D_MODEL), (N_HEADS * V_HEAD) ** -0.5),
        "w_out_all": nrm(ks[21], (DEPTH, D_MODEL, D_MODEL), D_MODEL ** -0.5),
        "ffn2_norm": gain(ks[22], D_MODEL),
        "ffn2_w_gate": nrm(ks[23], (DEPTH, D_MODEL, D_FF), D_MODEL ** -0.5),
        "ffn2_w_up": nrm(ks[24], (DEPTH, D_MODEL, D_FF), D_MODEL ** -0.5),
        "ffn2_w_down": nrm(ks[25], (DEPTH, D_FF, D_MODEL), D_FF ** -0.5),
    }


def reference(x_prompt, x_sample, cache_conv, cache_kv_latent, cache_k_rope, meta_tokens,
              ffn1_norm, ffn1_w_gate, ffn1_w_up, ffn1_w_down, mix_norm, w_in_all, conv_w,
              w_conv_out, q_a_norm, w_uq, kv_a_norm, w_ukv, q_norm, k_norm, w_mla_out,
              w_out_all, ffn2_norm, ffn2_w_gate, ffn2_w_up, ffn2_w_down):
    bsz_p = x_prompt.shape[0]
    meta = jnp.broadcast_to(meta_tokens[None].astype(x_prompt.dtype), (bsz_p, N_META, D_MODEL))
    xp = jnp.concatenate([meta, x_prompt], axis=1)
    xs = x_sample
    L_p, L_s = xp.shape[1], xs.shape[1]
    pos_p = jnp.arange(L_p, dtype=jnp.int32)
    pos_s = N_META + PAST_LEN + jnp.arange(L_s, dtype=jnp.int32)

    conv_p_rows, ckv_p_rows, kpe_p_rows = [], [], []
    conv_s_rows, ckv_s_rows, kpe_s_rows = [], [], []
    for l in range(DEPTH):
        xp = half_step_ffn(xp, ffn1_norm[l], ffn1_w_gate[l], ffn1_w_up[l], ffn1_w_down[l])
        xs = half_step_ffn(xs, ffn1_norm[l], ffn1_w_gate[l], ffn1_w_up[l], ffn1_w_down[l])

        bp, cin_p, qp, ckv_p, kpe_p, gcp, gmp = project_inputs(
            rms_norm(xp, mix_norm[l]), pos_p, w_in_all[l], q_a_norm[l], w_uq[l], kv_a_norm[l], q_norm[l])
        bs, cin_s, qs, ckv_s, kpe_s, gcs, gms = project_inputs(
            rms_norm(xs, mix_norm[l]), pos_s, w_in_all[l], q_a_norm[l], w_uq[l], kv_a_norm[l], q_norm[l])

        ext_p = jnp.pad(cin_p, ((0, 0), (CONV_WIDTH - 1, 0), (0, 0)))
        conv_y_p = depthwise_causal_conv(ext_p, conv_w[l])
        ext_s = jnp.concatenate([cache_conv[l].astype(cin_s.dtype), cin_s], axis=1)
        conv_y_s = depthwise_causal_conv(ext_s, conv_w[l])

        k_p, v_p = expand_keys(ckv_p, kpe_p, w_ukv[l], k_norm[l])
        attn_p = prompt_attention(qp, k_p, v_p)
        ckv_all = jnp.concatenate([cache_kv_latent[l].astype(ckv_s.dtype), ckv_s], axis=1)
        kpe_all = jnp.concatenate([cache_k_rope[l].astype(kpe_s.dtype), kpe_s], axis=1)
        k_s, v_s = expand_keys(ckv_all, kpe_all, w_ukv[l], k_norm[l])
        attn_s = attend(qs, k_s, v_s, None).reshape(xs.shape[0], L_s, N_HEADS * V_HEAD)

        xp = merge_branches(xp, bp, conv_y_p, attn_p, gcp, gmp, w_conv_out[l], w_mla_out[l], w_out_all[l])
        xs = merge_branches(xs, bs, conv_y_s, attn_s, gcs, gms, w_conv_out[l], w_mla_out[l], w_out_all[l])

        xp = half_step_ffn(xp, ffn2_norm[l], ffn2_w_gate[l], ffn2_w_up[l], ffn2_w_down[l])
        xs = half_step_ffn(xs, ffn2_norm[l], ffn2_w_gate[l], ffn2_w_up[l], ffn2_w_down[l])

        conv_p_rows.append(ext_p[:, -(CONV_WIDTH - 1):])
        ckv_p_rows.append(ckv_p)
        kpe_p_rows.append(kpe_p)
        conv_s_rows.append(ext_s[:, -(CONV_WIDTH - 1):])
        ckv_s_rows.append(ckv_s)
        kpe_s_rows.append(kpe_s)

    y_prompt = xp[:, N_META:]
    y_sample = xs
    new_conv_prompt = jnp.stack(conv_p_rows)
    new_kv_latent_prompt = jnp.stack(ckv_p_rows)
    new_k_rope_prompt = jnp.stack(kpe_p_rows)
    new_conv_sample = jnp.stack(conv_s_rows)
    new_kv_latent_sample = jnp.stack(ckv_s_rows)
    new_k_rope_sample = jnp.stack(kpe_s_rows)
    return (y_prompt, y_sample, new_conv_prompt, new_kv_latent_prompt, new_k_rope_prompt,
            new_conv_sample, new_kv_latent_sample, new_k_rope_sample)
```

```python
import numpy as np
import concourse.bass as bass
import concourse.mybir as mybir
from concourse.bass_utils import run_bass_kernel_spmd

F32 = mybir.dt.float32
BF16 = mybir.dt.bfloat16
AF = mybir.ActivationFunctionType
ALU = mybir.AluOpType

NCORES = 8
D = 1024
FF = 2816
NJ = FF // 128
SEQ = 4096
NMETA = 16
LP = SEQ + NMETA
DEC = 32
PAST = 2048
T = 512
NT = SEQ // T
NH = 16
QK = 96
EPS = 1e-6
NSPEC = NMETA + 2 * DEC
NCOL = NSPEC + SEQ
DBG = {}

U_C, U_V, U_B, U_Q, U_GC, U_GM = 0, 8, 16, 24, 27, 35
NWIN = 43


class Res:
    __slots__ = ("w", "rs", "name")

    def __init__(self, name=""):
        self.w = None
        self.rs = []
        self.name = name


class Op:
    __slots__ = ("eng", "fn", "deps", "inc", "val", "is_dma", "chan", "ci", "dead", "idx")


class Chan:
    def __init__(self, name, n):
        self.name = name
        self.n = n
        self.k = 0
        self.count = [0] * n
        self.last = [None] * n
        self.sems = [None] * n


ENGS = ("pe", "act", "dve", "pool", "sp")


class Prog:
    def __init__(self, nc):
        self.nc = nc
        self.lists = {e: [] for e in ENGS}
        self.chans = []

    def chan(self, name, n):
        c = Chan(name, n)
        self.chans.append(c)
        return c

    def op(self, eng, fn, reads=(), writes=(), chan=None):
        o = Op()
        o.eng = eng
        o.fn = fn
        o.inc = False
        o.val = 0
        o.is_dma = chan is not None
        o.chan = chan
        o.ci = 0
        o.dead = False
        raw = []
        oth = []
        for r in reads:
            if r.w is not None:
                raw.append(r.w)
        for w in writes:
            if w.w is not None:
                oth.append(w.w)
            oth.extend(w.rs)
        deps = {}
        for d in raw:
            if d is o:
                continue
            if (not d.is_dma) and (not o.is_dma) and d.eng == eng and eng == "pe":
                continue
            deps[id(d)] = d
        for d in oth:
            if d is o:
                continue
            if (not d.is_dma) and (not o.is_dma) and d.eng == eng and eng == "pe":
                continue
            deps[id(d)] = d
        if chan is not None:
            i = chan.k % chan.n
            chan.k += 1
            if chan.last[i] is not None:
                deps[id(chan.last[i])] = chan.last[i]
            chan.count[i] += 16
            o.ci = i
            o.val = chan.count[i]
            chan.last[i] = o
        best = {}
        final = []
        for d in deps.values():
            if d.is_dma:
                final.append(d)
            elif d.eng not in best or d.idx > best[d.eng].idx:
                best[d.eng] = d
        final.extend(best.values())
        o.deps = final
        o.idx = len(self.lists[eng])
        for d in o.deps:
            d.inc = True
        for r in reads:
            r.rs.append(o)
        for w in writes:
            w.w = o
            w.rs = []
        self.lists[eng].append(o)
        return o

    def pe(self, fn, reads=(), writes=()):
        return self.op("pe", fn, reads, writes)

    def act(self, fn, reads=(), writes=()):
        return self.op("act", fn, reads, writes)

    def dve(self, fn, reads=(), writes=()):
        return self.op("dve", fn, reads, writes)

    def pool(self, fn, reads=(), writes=()):
        return self.op("pool", fn, reads, writes)

    def dma(self, eng, chan, out, in_, reads=(), writes=(), **kw):
        return self.op(eng, lambda e: e.dma_start(out=out, in_=in_, **kw), reads, writes, chan=chan)

    def check(self):
        ptr = {e: 0 for e in ENGS}
        done = set()
        lists = {e: [o for o in self.lists[e] if not o.dead] for e in ENGS}
        progress = True
        while progress:
            progress = False
            for e in ENGS:
                L = lists[e]
                while ptr[e] < len(L):
                    o = L[ptr[e]]
                    if all((d.dead or id(d) in done) for d in o.deps):
                        done.add(id(o))
                        ptr[e] += 1
                        progress = True
                    else:
                        break
        stuck = {e: (ptr[e], len(lists[e])) for e in ENGS if ptr[e] < len(lists[e])}
        assert not stuck, "dependency deadlock: %r" % (stuck,)

    def emit(self):
        self.check()
        nc = self.nc
        import contextlib

        with contextlib.ExitStack() as st:
            esem = {}
            for e in ("pe", "act", "dve", "pool"):
                esem[e] = st.enter_context(nc.semaphore("s_" + e))
            for c in self.chans:
                for i in range(c.n):
                    c.sems[i] = st.enter_context(nc.semaphore("c_%s_%d" % (c.name, i)))
            for e in ("pe", "act", "dve", "pool"):
                cnt = 0
                for o in self.lists[e]:
                    if o.is_dma or o.dead:
                        continue
                    if o.inc:
                        cnt += 1
                        o.val = cnt
            block = st.enter_context(nc.Block())

            def run(engname, eng, final=False):
                seen = {}
                for o in self.lists[engname]:
                    if o.dead:
                        continue
                    for d in o.deps:
                        if d.dead:
                            continue
                        if d.is_dma:
                            sem = d.chan.sems[d.ci]
                        else:
                            sem = esem[d.eng]
                        key = id(sem)
                        if seen.get(key, 0) < d.val:
                            eng.wait_ge(sem, d.val)
                            seen[key] = d.val
                    ins = o.fn(eng)
                    if o.is_dma:
                        ins.then_inc(o.chan.sems[o.ci], 16)
                    elif o.inc:
                        ins.then_inc(esem[engname], 1)
                if final:
                    fin = {}
                    for en in ENGS:
                        for o in self.lists[en]:
                            if o.is_dma and not o.dead:
                                k = (id(o.chan), o.ci)
                                fin[k] = max(fin.get(k, 0), o.val)
                    for c in self.chans:
                        for i in range(c.n):
                            v = fin.get((id(c), i), 0)
                            if v > 0:
                                eng.wait_ge(c.sems[i], v)

            @block.tensor
            def _(e):
                run("pe", e)

            @block.scalar
            def _(e):
                run("act", e)

            @block.vector
            def _(e):
                run("dve", e)

            @block.gpsimd
            def _(e):
                run("pool", e)

            @block.sync
            def _(e):
                run("sp", e, final=True)


class BankPool:
    def __init__(self, banks):
        self.banks = banks
        self.freeq = list(range(len(banks)))

    def get(self):
        assert self.freeq, "PSUM pool exhausted"
        i = self.freeq.pop(0)
        t, r = self.banks[i]
        return (t, r, i)

    def free(self, b):
        assert b[2] not in self.freeq
        self.freeq.append(b[2])


class Ring:
    def __init__(self, items):
        self.items = items
        self.k = 0

    def get(self):
        it = self.items[self.k % len(self.items)]
        self.k += 1
        return it


class Stream:
    def __init__(self, P, name, slots, chan):
        self.P = P
        self.slots = slots
        self.n = 0
        self.chan = chan
        self.pending = {}
        self.released = set()
        self.limit = None

    def _issue(self, n, dep_reads):
        slot_t, slot_r = self.slots[n % len(self.slots)]
        cell = {"parts": None}

        def fn(e, cell=cell):
            ins = None
            for (o, i, kw) in cell["parts"]:
                ins = e.dma_start(out=o, in_=i, **kw)
            return ins

        o = self.P.op("sp", fn, reads=dep_reads, writes=[slot_r], chan=self.chan)
        o.dead = True
        self.pending[n] = (o, cell)

    def get(self, make_parts, src_res):
        n = self.n
        self.n += 1
        if n not in self.pending:
            assert n < len(self.slots) or (n - len(self.slots)) in self.released, "stream ring too small"
            self._issue(n, [])
        o, cell = self.pending.pop(n)
        slot_t, slot_r = self.slots[n % len(self.slots)]
        parts = make_parts(slot_t)
        assert len(parts) == 1
        cell["parts"] = parts
        o.dead = False
        for r in src_res:
            if r.w is not None and all(d is not r.w for d in o.deps):
                o.deps.append(r.w)
                r.w.inc = True
        return slot_t, slot_r, n

    def done(self, n):
        self.released.add(n)
        nxt = n + len(self.slots)
        if self.limit is None or nxt < self.limit:
            self._issue(nxt, [])


def build_program(stg=99):
    nc = bass.Bass("TRN2", target_bir_lowering=False)
    P = Prog(nc)
    import contextlib

    st = contextlib.ExitStack()

    def din(name, shape, dt=F32):
        return nc.dram_tensor(name, list(shape), dt, kind="ExternalInput").ap()

    def dout(name, shape):
        return nc.dram_tensor(name, list(shape), F32, kind="ExternalOutput").ap()

    def dscr(name, shape, dt=BF16):
        return nc.dram_tensor(name, list(shape), dt, kind="Internal").ap()

    x_p = din("x_p", [2, SEQ, D])
    x_s = din("x_s", [2, DEC, D])
    c_conv = din("c_conv", [2, 2, D])
    c_kv = din("c_kv", [2, PAST, 128])
    c_kr = din("c_kr", [2, PAST, 32])
    meta = din("meta", [NMETA, D])
    wg_in = [din("wg1", [NJ, 128, 1024]), din("wg2", [NJ, 128, 1024])]
    wu_in = [din("wu1", [NJ, 128, 1024]), din("wu2", [NJ, 128, 1024])]
    wd_in = [din("wd1", [FF, D]), din("wd2", [FF, D])]
    win_in = din("win_u", [NWIN, 128, 1024])
    wkv_in = din("wkv", [128, 8 * 160])
    wco_in = din("wco_u", [8, 128, 1024])
    wmo_in = din("wmo_u", [8, 128, 1024])
    woa_in = din("woa", [D, D])
    wq_in = din("wq_p", [128, 3 * NH * 128])
    wka_in = din("wka", [128, NH * 128])
    wuv_in = din("wuv", [128, 1024])
    bsel_in = din("bsel", [32, 128])
    ident_in = din("ident", [128, 128])
    gT_in = din("gT", [128, 24])
    gqa_in = din("gqaT", [128, 3])
    gkv_in = din("gkv_bc", [128, 128])
    gq_in = din("gq_p", [96, 1])
    gk_in = din("gk_p", [96, 1])
    cw_in = din("cwT", [128, 24])
    csfm_in = din("cs_fm", [128, NCOL])
    cstm_in = din("cs_tm", [NCOL, 64])

    y_p = dout("y_p", [2, SEQ, D])
    y_s = dout("y_s", [2, DEC, D])
    nconv_p = dout("nconv_p", [2, 2, D])
    nkv_p = dout("nkv_p", [2, LP, 128])
    nkr_p = dout("nkr_p", [2, LP, 32])
    nconv_s = dout("nconv_s", [2, 2, D])
    nkv_s = dout("nkv_s", [2, DEC, 128])
    nkr_s = dout("nkr_s", [2, DEC, 32])

    wg_s = [dscr("wg1_s", [NJ, 128, 1024]), dscr("wg2_s", [NJ, 128, 1024])]
    wu_s = [dscr("wu1_s", [NJ, 128, 1024]), dscr("wu2_s", [NJ, 128, 1024])]
    wd_s = [dscr("wd1_s", [FF, D]), dscr("wd2_s", [FF, D])]
    win_s = dscr("win_s", [NWIN, 128, 1024])
    wco_s = dscr("wco_s", [8, 128, 1024])
    wmo_s = dscr("wmo_s", [8, 128, 1024])
    woa_s = dscr("woa_s", [D, D])
    kS = dscr("kS", [2, NT, NH, 96, T])
    vS = dscr("vS", [2, NT, NH, 128, T])

    def sb(name, shape, dt=F32):
        return st.enter_context(nc.sbuf_tensor("S_" + name, list(shape), dt))

    NSLOT = 8
    wslots = [(sb("wsl%d" % i, [128, 1024], BF16), Res("wsl%d" % i)) for i in range(NSLOT)]
    NKV = 5
    kvslots = [(sb("kvsl%d" % i, [128, T], BF16), Res("kvsl%d" % i)) for i in range(NKV)]
    xbufs = []
    for xi in range(2):
        xbufs.append((sb("xt%d" % xi, [128, 4, D]),
                      [[Res("xt%d_%d_%d" % (xi, s, h)) for h in range(2)] for s in range(4)]))
    X = {"t": xbufs[0][0], "R": xbufs[0][1]}
    xn = [(sb("xn%d" % i, [128, D], BF16), Res("xn%d" % i)) for i in range(2)]
    xnR = Ring(xn)
    R_junk = Res("junk")
    hT = sb("hT", [128, 8, T], BF16)
    R_hT = [Res("hT%d" % k) for k in range(8)]
    arena = sb("arena", [128, 2 * NH * T], BF16)
    aT = arena[:, 0:NJ * T].rearrange("p (j t) -> p j t", t=T)
    qT = arena[:, 0:NH * T].rearrange("p (h t) -> p h t", t=T)
    kTn = arena[:, NH * T:2 * NH * T].rearrange("p (h t) -> p h t", t=T)
    R_aT = [Res("aT%d" % j) for j in range(NJ)]
    R_qT = [Res("qT%d" % h) for h in range(NH)]
    R_kT = [Res("kT%d" % h) for h in range(NH)]
    vn = sb("vn", [128, NH, 4, 128], BF16)
    R_vn = Res("vn")
    gated = sb("gated", [128, 8, T], BF16)
    R_gated = [Res("gated%d" % k) for k in range(8)]
    attnT = sb("attnT", [128, 8, T], BF16)
    R_attn = [Res("attn%d" % k) for k in range(8)]
    mergedT = arena[:, 0:8 * T].rearrange("p (k t) -> p k t", t=T)
    R_merged = [Res("merged%d" % k) for k in range(8)]
    sqb = sb("sqb", [128, 3, T], BF16)
    R_sqb = Res("sqb")
    junk = sqb[:].rearrange("p a b -> p (a b)")
    qlT = sb("qlT", [128, 3, T], BF16)
    R_qlT = Res("qlT")
    ckvT = sb("ckvT", [128, T], BF16)
    R_ckvT = Res("ckvT")
    kpeT = sb("kpeT", [32, T], BF16)
    R_kpeT = Res("kpeT")
    kvt = sb("kvt", [128, 4, 160])
    R_kvt = Res("kvt")
    ckv_tm = sb("ckv_tm", [128, 4, 128])
    R_ckvtm = Res("ckv_tm")
    kpe_tm = sb("kpe_tm", [128, 4, 32])
    R_kpetm = Res("kpe_tm")
    rt1 = sb("rt1", [128, 4, 32])
    rt2 = sb("rt2", [128, 4, 32])
    R_rt = Res("rt")
    kvb = sb("kvb", [128, 4, 160], BF16)
    R_kvb = Res("kvb")
    tmpf = [(sb("tmpf%d" % i, [128, T]), Res("tmpf%d" % i)) for i in range(5)]
    tmpR = Ring(tmpf)
    qbf = [(sb("qbf%d" % i, [128, T]), Res("qbf%d" % i)) for i in range(4)]
    qbR = Ring(qbf)
    sqh = [(sb("sqh%d" % i, [96, T], BF16), Res("sqh%d" % i)) for i in range(6)]
    sqhR = Ring(sqh)
    pTb = [(sb("pT%d" % i, [128, T], BF16), Res("pT%d" % i)) for i in range(5)]
    pTR = Ring(pTb)
    cinb = [(sb("cin%d" % i, [128, T + 8]), Res("cin%d" % i)) for i in range(2)]
    cinR = Ring(cinb)
    halo = sb("halo", [128, 8, 2])
    R_halo = Res("halo")
    halo_meta = sb("halo_meta", [128, 8, 2])
    R_halo_meta = Res("halo_meta")
    cch = sb("cch", [128, 8, 2, 2])
    R_cch = Res("cch")
    ncs = sb("ncs", [128, 8, 2, 2])
    R_ncs = Res("ncs")
    ss4 = sb("ss4", [128, 8])
    R_ss4 = Res("ss4")
    kT_meta = sb("kT_meta", [96, NH, NMETA], BF16)
    R_kTm = Res("kT_meta")
    v_meta = sb("v_meta", [NMETA, NH, 128], BF16)
    R_vm = Res("v_meta")
    cs_f = sb("cs_f", [128, T])
    R_csf = Res("cs_f")
    cs_t = sb("cs_t", [128, 4, 64])
    R_cst = Res("cs_t")
    stage = kvt
    R_stage = R_kvt
    wq = sb("wq", [128, 3, NH, 128], BF16)
    wka = sb("wka", [128, NH, 128], BF16)
    wuv = sb("wuv", [128, 1024], BF16)
    wkv = sb("wkv", [128, 8, 160], BF16)
    bsel = sb("bsel", [32, 128], BF16)
    ident = sb("ident", [128, 128], BF16)
    ones = sb("ones", [128, 128], BF16)
    gT = sb("gT", [128, 24])
    gqa = sb("gqa", [128, 3])
    gkv = sb("gkv", [128, 128])
    gq = sb("gq", [96, 1])
    gk = sb("gk", [96, 1])
    cw = sb("cw", [128, 8, 3])
    R_const = Res("const")

    psum = [(st.enter_context(nc.psum_tensor("ps%d" % i, [128, T], F32)), Res("ps%d" % i)) for i in range(8)]
    PA = BankPool(psum[0:6])
    PB = BankPool(psum[6:8])

    ch_w = P.chan("w", NSLOT)
    ch_kv = P.chan("kv", NKV)
    ch_cast = P.chan("cast", 48)
    ch_io = P.chan("io", 8)
    ch_c = P.chan("const", 8)

    WS = Stream(P, "w", wslots, ch_w)
    KVS = Stream(P, "kv", kvslots, ch_kv)

    scr_res = {}

    def sres(key):
        if key not in scr_res:
            scr_res[key] = Res(str(key))
        return scr_res[key]

    def cload(dst, src, cast=False):
        P.dma("pool", ch_c, dst, src, writes=[R_const])

    cload(gT[:], gT_in)
    cload(gqa[:], gqa_in)
    cload(gkv[:], gkv_in)
    cload(gq[:], gq_in)
    cload(gk[:], gk_in)
    cload(cw[:].rearrange("p a b -> p (a b)"), cw_in)
    cload(ident[:], ident_in)
    cload(bsel[:], bsel_in)
    cload(wq[:].rearrange("p a b c -> p (a b c)"), wq_in)
    cload(wka[:].rearrange("p a b -> p (a b)"), wka_in)
    cload(wuv[:], wuv_in)
    cload(wkv[:].rearrange("p a b -> p (a b)"), wkv_in)
    with nc.allow_non_contiguous_dma(reason="tiny transposed cache_conv load"):
        for sbi in range(2):
            for r in range(2):
                P.dma("pool", ch_c, cch[:, :, sbi, r],
                      c_conv[sbi, r, :].rearrange("(c p) -> p c", p=128), writes=[R_const],
                      allow_slow_non_contiguous=True)
    P.pool(lambda e: e.memset(ones[:], 1.0), writes=[R_const])
    P.pool(lambda e: e.memset(vn[:].rearrange("p a b c -> p (a b c)"), 1.0), writes=[R_vn])
    P.pool(lambda e: e.memset(v_meta[:].rearrange("p a b -> p (a b)"), 1.0), writes=[R_vm])
    P.pool(lambda e: e.memset(attnT[:].rearrange("p a b -> p (a b)"), 0.0), writes=R_attn)
    P.pool(lambda e: e.memset(halo_meta[:].rearrange("p a b -> p (a b)"), 0.0), writes=[R_halo_meta])
    for c3 in range(3):
        P.dve(lambda e, c3=c3: e.tensor_scalar(out=wq[:, c3, :, 96:112], in0=wq[:, c3, :, 96:112], scalar1=-1.0,
                                               scalar2=None, op0=ALU.mult), reads=[R_const], writes=[R_const])
    P.dve(lambda e: e.tensor_scalar(out=gq[:], in0=gq[:], scalar1=float(QK ** -0.5), scalar2=None,
                                    op0=ALU.mult), reads=[R_const], writes=[R_const])

    def cast(dst, src, key):
        P.dma("pool", ch_cast, dst, src, writes=[sres(key)])

    def cast_ffn(f):
        for j in range(NJ):
            cast(wg_s[f][j], wg_in[f][j], ("wg", f, j))
            cast(wu_s[f][j], wu_in[f][j], ("wu", f, j))
        for j in range(NJ):
            cast(wd_s[f][j * 128:(j + 1) * 128, :], wd_in[f][j * 128:(j + 1) * 128, :], ("wd", f, j))

    def wunit(ap2d, keys):
        def mk(slot):
            if len(ap2d.shape) == 3:
                o = slot[:].rearrange("p (a b) -> p a b", a=ap2d.shape[1])
            else:
                o = slot[:]
            return [(o, ap2d, {})]
        return WS.get(mk, [sres(k) for k in keys])

    def mm(ps, M, N0, N1, lhsT, rhs, start, stop, reads, writes_res, kpart=None):
        t = ps[0]
        return P.pe(lambda e: e.matmul(t[0:M, N0:N1], lhsT, rhs, start=start, stop=stop),
                    reads=reads, writes=[writes_res])

    class TileInfo:
        pass

    def make_tile(kind, b=0, i=0):
        ti = TileInfo()
        ti.kind = kind
        ti.b = b
        ti.i = i
        if kind == "spec":
            ti.n = NSPEC
            ti.subs = [NSPEC]
            ti.segs = [(0, NMETA, "zero"), (NMETA, DEC, "c0"), (NMETA + DEC, DEC, "c1")]
            ti.col0 = 0
            ti.vblk = [(0, NMETA), (NMETA, DEC), (NMETA + DEC, DEC)]
        else:
            ti.n = T
            ti.subs = [128] * 4
            ti.segs = [(0, T, "meta" if i == 0 else "prev")]
            ti.col0 = NSPEC + i * T
            ti.vblk = [(s * 128, 128) for s in range(4)]
        return ti

    def rms_to_hT(ti, gcol):
        xt, R_xt = X["t"], X["R"]
        n = ti.n
        ns = len(ti.subs)
        nr0 = ti.subs[0]
        P.pool(lambda e: e.memset(ss4[:, 0:4], 0.0), writes=[R_ss4])
        for s, nr in enumerate(ti.subs):
            P.act(lambda e, s=s, nr=nr: e.activation(out=junk[0:nr, 0:D], in_=xt[0:nr, s, :], func=AF.Square,
                                                     accum_out=ss4[0:nr, s:s + 1]),
                  reads=[R_xt[s][0], R_xt[s][1]], writes=[R_junk, R_ss4])
        P.act(lambda e: e.activation(out=ss4[0:nr0, 4:4 + ns], in_=ss4[0:nr0, 0:ns], func=AF.Ln,
                                     scale=1.0 / D, bias=EPS), reads=[R_ss4], writes=[R_ss4])
        P.act(lambda e: e.activation(out=ss4[0:nr0, 0:ns], in_=ss4[0:nr0, 4:4 + ns], func=AF.Exp, scale=-0.5),
              reads=[R_ss4], writes=[R_ss4])
        banks = [PA.get() for _ in range(6)] + [PB.get() for _ in range(2)]
        for s, nr in enumerate(ti.subs):
            xb, xr = xnR.get()
            if s % 2 == 0:
                P.dve(lambda e, s=s, nr=nr, xb=xb: e.tensor_scalar(out=xb[0:nr, :], in0=xt[0:nr, s, :],
                                                                   scalar1=ss4[0:nr, s:s + 1], scalar2=None,
                                                                   op0=ALU.mult),
                      reads=[R_xt[s][0], R_xt[s][1], R_ss4], writes=[xr])
            else:
                P.act(lambda e, s=s, nr=nr, xb=xb: e.activation(out=xb[0:nr, :], in_=xt[0:nr, s, :], func=AF.Copy,
                                                                scale=ss4[0:nr, s:s + 1]),
                      reads=[R_xt[s][0], R_xt[s][1], R_ss4], writes=[xr])
            for kc in range(8):
                bk = banks[kc]
                mm(bk, 128, s * 128, s * 128 + nr, xb[0:nr, kc * 128:(kc + 1) * 128], ident[0:nr, 0:nr],
                   True, True, [xr, R_const], bk[1])
        for kc in range(8):
            bk = banks[kc]
            if kc % 2 == 0:
                P.dve(lambda e, bk=bk, kc=kc: e.tensor_scalar(
                    out=hT[:, kc, 0:n], in0=bk[0][:, 0:n], scalar1=gT[:, gcol + kc:gcol + kc + 1],
                    scalar2=None, op0=ALU.mult), reads=[bk[1], R_const], writes=[R_hT[kc]])
            else:
                P.act(lambda e, bk=bk, kc=kc: e.activation(
                    out=hT[:, kc, 0:n], in_=bk[0][:, 0:n], func=AF.Copy,
                    scale=gT[:, gcol + kc:gcol + kc + 1]), reads=[bk[1], R_const], writes=[R_hT[kc]])
        for kc in range(6):
            PA.free(banks[kc])
        PB.free(banks[6])
        PB.free(banks[7])

    def ffn(ti, f, pre=False, mid_hook=None):
        xt, R_xt = X["t"], X["R"]
        n = ti.n
        if not pre:
            rms_to_hT(ti, 0 if f == 0 else 16)
        for j in range(NJ):
            ug, rg, ng = wunit(wg_s[f][j], [("wg", f, j)])
            uu, ru, nu = wunit(wu_s[f][j], [("wu", f, j)])
            pg = PA.get()
            pu = PA.get()
            ugv = ug[:].rearrange("p (k m) -> p k m", k=8)
            uuv = uu[:].rearrange("p (k m) -> p k m", k=8)
            for kc in range(8):
                mm(pg, 128, 0, n, ugv[:, kc, :], hT[:, kc, 0:n], kc == 0, kc == 7, [rg, R_hT[kc]], pg[1])
            WS.done(ng)
            for kc in range(8):
                mm(pu, 128, 0, n, uuv[:, kc, :], hT[:, kc, 0:n], kc == 0, kc == 7, [ru, R_hT[kc]], pu[1])
            WS.done(nu)
            tb, tr = tmpR.get()
            P.act(lambda e, tb=tb, pg=pg: e.activation(out=tb[:, 0:n], in_=pg[0][:, 0:n], func=AF.Silu),
                  reads=[pg[1]], writes=[tr])
            P.dve(lambda e, tb=tb, pu=pu, j=j: e.tensor_tensor(out=aT[:, j, 0:n], in0=pu[0][:, 0:n],
                                                               in1=tb[:, 0:n], op=ALU.mult),
                  reads=[pu[1], tr], writes=[R_aT[j]] + ((R_kT + R_qT) if j == 0 else []))
            PA.free(pg)
            PA.free(pu)
        if mid_hook is not None:
            mid_hook()
        for half in range(2):
            pd = [PA.get() for _ in ti.subs]
            for jj in range(NJ // 2):
                src = wd_s[f][jj * 256:(jj + 1) * 256, half * T:(half + 1) * T].rearrange("(t p) n -> p t n", p=128)
                u, ru, nu = wunit(src, [("wd", f, 2 * jj), ("wd", f, 2 * jj + 1)])
                uv = u[:].rearrange("p (t n) -> p t n", t=2)
                for t2 in range(2):
                    j = 2 * jj + t2
                    for s, nr in enumerate(ti.subs):
                        mm(pd[s], nr, 0, T, aT[:, j, s * 128:s * 128 + nr], uv[:, t2, :], j == 0, j == NJ - 1,
                           [ru, R_aT[j]], pd[s][1])
                WS.done(nu)
            for s, nr in enumerate(ti.subs):
                P.dve(lambda e, s=s, nr=nr, pd=pd, half=half: e.scalar_tensor_tensor(
                    out=xt[0:nr, s, half * T:(half + 1) * T], in0=pd[s][0][0:nr, :], scalar=0.5,
                    in1=xt[0:nr, s, half * T:(half + 1) * T], op0=ALU.mult, op1=ALU.add),
                    reads=[pd[s][1], R_xt[s][half]], writes=[R_xt[s][half]])
                PA.free(pd[s])

    def rstd_from(ps, M, n, dim):
        lb, lr = tmpR.get()
        P.act(lambda e: e.activation(out=lb[0:M, 0:n], in_=ps[0][0:M, 0:n], func=AF.Ln, scale=1.0 / dim, bias=EPS),
              reads=[ps[1]], writes=[lr])
        rb, rr = tmpR.get()
        P.act(lambda e: e.activation(out=rb[0:M, 0:n], in_=lb[0:M, 0:n], func=AF.Exp, scale=-0.5),
              reads=[lr], writes=[rr])
        return rb, rr

    def make_k_stages(n, pssP):
        stash = {}

        def k_stage1(h):
            pk = PA.get()
            mm(pk, 128, 0, n, wka[:, h, :], ckvT[:, 0:n], True, False, [R_ckvT, R_const], pk[1])
            mm(pk, 128, 0, n, bsel[:, :], kpeT[:, 0:n], False, True, [R_kpeT, R_const], pk[1])
            sq, sr = sqhR.get()
            P.act(lambda e, sq=sq, pk=pk: e.activation(out=sq[:, 0:n], in_=pk[0][0:96, 0:n], func=AF.Square),
                  reads=[pk[1]], writes=[sr])
            stash[h] = (pk, sq, sr)

        def k_stage2(h):
            pk, sq, sr = stash.pop(h)
            pss = pssP.get()
            mm(pss, 128, 0, n, ones[0:96, :], sq[:, 0:n], True, True, [sr, R_const], pss[1])
            rb, rr = rstd_from(pss, 96, n, 96)
            pssP.free(pss)
            P.dve(lambda e, pk=pk, rb=rb, h=h: e.scalar_tensor_tensor(
                out=kTn[0:96, h, 0:n], in0=pk[0][0:96, 0:n], scalar=gk[:, 0:1], in1=rb[0:96, 0:n],
                op0=ALU.mult, op1=ALU.mult), reads=[pk[1], rr, R_const], writes=[R_kT[h]])
            PA.free(pk)

        return k_stage1, k_stage2

    def kv_prep(n, vblk, pssP=None):
        k_stage1, k_stage2 = make_k_stages(n, pssP or PB)
        LK = 2
        for h in range(NH + LK):
            if h < NH:
                k_stage1(h)
            if h - LK >= 0:
                k_stage2(h - LK)
        v_part(n, vblk)

    def v_part(n, vblk):
        for bi, (c0, ln) in enumerate(vblk):
            for half in range(2):
                pv = PA.get()
                mm(pv, ln, 0, T, ckvT[:, c0:c0 + ln], wuv[:, half * T:(half + 1) * T], True, True,
                   [R_ckvT, R_const], pv[1])
                dst = vn[0:ln, half * 8:(half + 1) * 8, bi, :].rearrange("p (a b) c -> p a b c", b=2)
                srcv = pv[0][0:ln, :].rearrange("p (a b d) -> p a b d", a=4, b=2)
                for par in range(2):
                    if (half + par) % 2 == 0:
                        P.act(lambda e, dst=dst, srcv=srcv, par=par: e.activation(
                            out=dst[:, :, par, par * 64:par * 64 + 64], in_=srcv[:, :, par, :], func=AF.Copy),
                            reads=[pv[1]], writes=[R_vn])
                    else:
                        P.dve(lambda e, dst=dst, srcv=srcv, par=par: e.tensor_copy(
                            out=dst[:, :, par, par * 64:par * 64 + 64], in_=srcv[:, :, par, :]),
                            reads=[pv[1]], writes=[R_vn])
                PA.free(pv)

    def kv_transpose(subs):
        b1 = PA.get()
        b2 = PA.get()
        t1 = b1[0][:].bitcast(BF16)
        t2 = b2[0][:].bitcast(BF16)
        n = 0
        for s, nr in enumerate(subs):
            P.pe(lambda e, s=s, nr=nr: e.transpose(t1[:, s * 128:s * 128 + nr], kvb[0:nr, s, 0:128], ident[0:nr, 0:nr]),
                 reads=[R_kvb, R_const], writes=[b1[1]])
            P.pe(lambda e, s=s, nr=nr: e.transpose(t2[0:32, s * 128:s * 128 + nr], kvb[0:nr, s, 128:160],
                                                   ident[0:nr, 0:nr]),
                 reads=[R_kvb, R_const], writes=[b2[1]])
            n = s * 128 + nr
        P.act(lambda e: e.activation(out=ckvT[:, 0:n], in_=t1[:, 0:n], func=AF.Copy), reads=[b1[1]], writes=[R_ckvT])
        P.dve(lambda e: e.tensor_copy(out=kpeT[:, 0:n], in_=t2[0:32, 0:n]), reads=[b2[1]], writes=[R_kpeT])
        PA.free(b1)
        PA.free(b2)

    def in_all(ti):
        n = ti.n
        rms_to_hT(ti, 8)
        P.dma("pool", ch_io, cs_f[64:128, 0:n], csfm_in[64:128, ti.col0:ti.col0 + n], writes=[R_csf])
        if ti.kind == "spec":
            P.dma("pool", ch_io, cs_t[0:n, 0, :], cstm_in[0:n, :], writes=[R_cst])
        else:
            P.dma("pool", ch_io, cs_t[:], cstm_in[ti.col0:ti.col0 + n, :].rearrange("(s p) c -> p s c", p=128),
                  writes=[R_cst])
        def conv_chunk(jc):
            uc, rc, n1 = wunit(win_s[U_C + jc], [("win", U_C + jc)])
            uv_, rv, n2 = wunit(win_s[U_V + jc], [("win", U_V + jc)])
            ub, rb_, n3 = wunit(win_s[U_B + jc], [("win", U_B + jc)])
            pc = PA.get()
            pv = PA.get()
            pb = PA.get()
            for (u, ru, pp, nn) in ((uc, rc, pc, n1), (uv_, rv, pv, n2), (ub, rb_, pb, n3)):
                uvw = u[:].rearrange("p (k m) -> p k m", k=8)
                for kc in range(8):
                    mm(pp, 128, 0, n, uvw[:, kc, :], hT[:, kc, 0:n], kc == 0, kc == 7, [ru, R_hT[kc]], pp[1])
                WS.done(nn)
            tb, tr = tmpR.get()
            P.act(lambda e, tb=tb, pc=pc: e.activation(out=tb[:, 0:n], in_=pc[0][:, 0:n], func=AF.Copy),
                  reads=[pc[1]], writes=[tr])
            PA.free(pc)
            cb, cr = cinR.get()
            yb, yr = tmpR.get()
            for si, (c0, ln, hk) in enumerate(ti.segs):
                o0 = c0 + 2 * si
                if hk == "zero":
                    P.pool(lambda e, cb=cb, o0=o0: e.memset(cb[:, o0:o0 + 2], 0.0), writes=[cr])
                elif hk in ("c0", "c1"):
                    sbi = 0 if hk == "c0" else 1
                    P.pool(lambda e, cb=cb, o0=o0, sbi=sbi, jc=jc: e.tensor_copy(out=cb[:, o0:o0 + 2], in_=cch[:, jc, sbi, :]),
                           reads=[R_const], writes=[cr])
                elif hk == "meta":
                    P.pool(lambda e, cb=cb, o0=o0, jc=jc: e.tensor_copy(out=cb[:, o0:o0 + 2], in_=halo_meta[:, jc, :]),
                           reads=[R_halo_meta], writes=[cr])
                else:
                    P.pool(lambda e, cb=cb, o0=o0, jc=jc: e.tensor_copy(out=cb[:, o0:o0 + 2], in_=halo[:, jc, :]),
                           reads=[R_halo], writes=[cr])
                P.dve(lambda e, cb=cb, o0=o0, c0=c0, ln=ln, tb=tb, pv=pv: e.tensor_tensor(
                    out=cb[:, o0 + 2:o0 + 2 + ln], in0=pv[0][:, c0:c0 + ln], in1=tb[:, c0:c0 + ln], op=ALU.mult),
                    reads=[pv[1], tr], writes=[cr])
                P.dve(lambda e, cb=cb, o0=o0, c0=c0, ln=ln, yb=yb, jc=jc: e.tensor_scalar(
                    out=yb[:, c0:c0 + ln], in0=cb[:, o0:o0 + ln], scalar1=cw[:, jc, 0:1], scalar2=None, op0=ALU.mult),
                    reads=[cr, R_const], writes=[yr])
                for k in (1, 2):
                    P.dve(lambda e, cb=cb, o0=o0, c0=c0, ln=ln, yb=yb, jc=jc, k=k: e.scalar_tensor_tensor(
                        out=yb[:, c0:c0 + ln], in0=cb[:, o0 + k:o0 + k + ln], scalar=cw[:, jc, k:k + 1],
                        in1=yb[:, c0:c0 + ln], op0=ALU.mult, op1=ALU.add), reads=[cr, yr, R_const], writes=[yr])
                if ti.kind == "spec":
                    if hk == "zero":
                        P.pool(lambda e, cb=cb, o0=o0, ln=ln, jc=jc: e.tensor_copy(
                            out=halo_meta[:, jc, :], in_=cb[:, o0 + ln:o0 + ln + 2]), reads=[cr], writes=[R_halo_meta])
                    else:
                        sbi = 0 if hk == "c0" else 1
                        P.pool(lambda e, cb=cb, o0=o0, ln=ln, jc=jc, sbi=sbi: e.tensor_copy(
                            out=ncs[:, jc, sbi, :], in_=cb[:, o0 + ln:o0 + ln + 2]), reads=[cr], writes=[R_ncs])
                else:
                    P.pool(lambda e, cb=cb, o0=o0, ln=ln, jc=jc: e.tensor_copy(
                        out=halo[:, jc, :], in_=cb[:, o0 + ln:o0 + ln + 2]), reads=[cr], writes=[R_halo])
            PA.free(pv)
            P.dve(lambda e, yb=yb, pb=pb, jc=jc: e.tensor_tensor(out=gated[:, jc, 0:n], in0=pb[0][:, 0:n],
                                                                 in1=yb[:, 0:n], op=ALU.mult),
                  reads=[pb[1], yr], writes=[R_gated[jc]])
            PA.free(pb)
        qlfb = [tmpR.get() for _ in range(3)]
        for c in range(3):
            u, ru, nn = wunit(win_s[U_Q + c], [("win", U_Q + c)])
            pq = PA.get()
            uvw = u[:].rearrange("p (k m) -> p k m", k=8)
            for kc in range(8):
                mm(pq, 128, 0, n, uvw[:, kc, :], hT[:, kc, 0:n], kc == 0, kc == 7, [ru, R_hT[kc]], pq[1])
            WS.done(nn)
            P.act(lambda e, pq=pq, c=c: e.activation(out=qlfb[c][0][:, 0:n], in_=pq[0][:, 0:n], func=AF.Copy),
                  reads=[pq[1]], writes=[qlfb[c][1]])
            P.act(lambda e, pq=pq, c=c: e.activation(out=sqb[:, c, 0:n], in_=pq[0][:, 0:n], func=AF.Square),
                  reads=[pq[1]], writes=[R_sqb])
            PA.free(pq)
        pss = PA.get()
        for c in range(3):
            mm(pss, 128, 0, n, ones[:, :], sqb[:, c, 0:n], c == 0, c == 2, [R_sqb, R_const], pss[1])
        rb, rr = rstd_from(pss, 128, n, 384)
        PA.free(pss)
        for c in range(3):
            P.dve(lambda e, c=c, rb=rb: e.scalar_tensor_tensor(
                out=qlT[:, c, 0:n], in0=qlfb[c][0][:, 0:n], scalar=gqa[:, c:c + 1], in1=rb[:, 0:n],
                op0=ALU.mult, op1=ALU.mult), reads=[qlfb[c][1], rr, R_const], writes=[R_qlT])
        ns = len(ti.subs)
        nr0 = ti.subs[0]
        P.pool(lambda e: e.memset(ss4[:, 0:4], 0.0), writes=[R_ss4])
        for s, nr in enumerate(ti.subs):
            pkv = PA.get()
            for kc in range(8):
                mm(pkv, nr, 0, 160, hT[:, kc, s * 128:s * 128 + nr], wkv[:, kc, :], kc == 0, kc == 7,
                   [R_hT[kc], R_const], pkv[1])
            P.act(lambda e, s=s, nr=nr, pkv=pkv: e.activation(out=kvt[0:nr, s, :], in_=pkv[0][0:nr, 0:160], func=AF.Copy),
                  reads=[pkv[1]], writes=[R_kvt])
            P.act(lambda e, s=s, nr=nr, pkv=pkv: e.activation(out=junk[0:nr, 0:128], in_=pkv[0][0:nr, 0:128],
                                                              func=AF.Square, accum_out=ss4[0:nr, s:s + 1]),
                  reads=[pkv[1]], writes=[R_junk, R_ss4])
            PA.free(pkv)
        P.act(lambda e: e.activation(out=ss4[0:nr0, 4:4 + ns], in_=ss4[0:nr0, 0:ns], func=AF.Ln,
                                     scale=1.0 / 128, bias=EPS), reads=[R_ss4], writes=[R_ss4])
        P.act(lambda e: e.activation(out=ss4[0:nr0, 0:ns], in_=ss4[0:nr0, 4:4 + ns], func=AF.Exp, scale=-0.5),
              reads=[R_ss4], writes=[R_ss4])
        for s, nr in enumerate(ti.subs):
            P.dve(lambda e, s=s, nr=nr: e.scalar_tensor_tensor(
                out=ckv_tm[0:nr, s, :], in0=kvt[0:nr, s, 0:128], scalar=ss4[0:nr, s:s + 1], in1=gkv[0:nr, :],
                op0=ALU.mult, op1=ALU.mult), reads=[R_kvt, R_ss4, R_const], writes=[R_ckvtm])
        xv = kvt[0:nr0, 0:ns, 128:160]
        P.dve(lambda e: e.tensor_tensor(out=rt1[0:nr0, 0:ns, :], in0=xv, in1=cs_t[0:nr0, 0:ns, 0:32], op=ALU.mult),
              reads=[R_kvt, R_cst], writes=[R_rt])
        P.dve(lambda e: e.tensor_tensor(out=rt2[0:nr0, 0:ns, :], in0=xv, in1=cs_t[0:nr0, 0:ns, 32:64], op=ALU.mult),
              reads=[R_kvt, R_cst], writes=[R_rt])
        P.dve(lambda e: e.tensor_tensor(out=kpe_tm[0:nr0, 0:ns, 0:16], in0=rt1[0:nr0, 0:ns, 0:16],
                                        in1=rt2[0:nr0, 0:ns, 16:32], op=ALU.subtract),
              reads=[R_rt], writes=[R_kpetm])
        P.dve(lambda e: e.tensor_tensor(out=kpe_tm[0:nr0, 0:ns, 16:32], in0=rt1[0:nr0, 0:ns, 16:32],
                                        in1=rt2[0:nr0, 0:ns, 0:16], op=ALU.add),
              reads=[R_rt], writes=[R_kpetm])
        P.pool(lambda e: e.tensor_copy(out=kvb[0:nr0, 0:ns, 0:128], in_=ckv_tm[0:nr0, 0:ns, :]),
               reads=[R_ckvtm], writes=[R_kvb])
        P.pool(lambda e: e.tensor_copy(out=kvb[0:nr0, 0:ns, 128:160], in_=kpe_tm[0:nr0, 0:ns, :]),
               reads=[R_kpetm], writes=[R_kvb])
        if ti.kind == "spec":
            for b in range(2):
                P.dma("pool", ch_io, nkv_p[b, 0:NMETA, :], ckv_tm[0:NMETA, 0, :], reads=[R_ckvtm])
                P.dma("pool", ch_io, nkr_p[b, 0:NMETA, :], kpe_tm[0:NMETA, 0, :], reads=[R_kpetm])
                r0 = NMETA + DEC * b
                P.dma("pool", ch_io, nkv_s[b], ckv_tm[r0:r0 + DEC, 0, :], reads=[R_ckvtm])
                P.dma("pool", ch_io, nkr_s[b], kpe_tm[r0:r0 + DEC, 0, :], reads=[R_kpetm])
        else:
            r0 = NMETA + ti.i * T
            P.dma("pool", ch_io, nkv_p[ti.b, r0:r0 + T, :].rearrange("(s p) c -> p s c", p=128), ckv_tm[:],
                  reads=[R_ckvtm])
            P.dma("pool", ch_io, nkr_p[ti.b, r0:r0 + T, :].rearrange("(s p) c -> p s c", p=128), kpe_tm[:],
                  reads=[R_kpetm])
        kv_transpose(ti.subs)
        LQ = 2
        qst = {}

        def q_stage1(h):
            pq = PA.get()
            for c in range(3):
                mm(pq, 128, 0, n, wq[:, c, h, :], qlT[:, c, 0:n], c == 0, c == 2, [R_qlT, R_const], pq[1])
            qb, qr = qbR.get()
            t2, t2r = tmpR.get()
            P.act(lambda e, qb=qb, pq=pq: e.activation(out=qb[0:96, 0:n], in_=pq[0][0:96, 0:n], func=AF.Copy),
                  reads=[pq[1]], writes=[qr])
            P.dve(lambda e, qb=qb, pq=pq: e.tensor_tensor(out=qb[64:96, 0:n], in0=pq[0][64:96, 0:n],
                                                          in1=cs_f[64:96, 0:n], op=ALU.mult),
                  reads=[pq[1], R_csf], writes=[qr])
            P.dve(lambda e, t2=t2, pq=pq: e.tensor_tensor(out=t2[64:96, 0:n], in0=pq[0][96:128, 0:n],
                                                          in1=cs_f[96:128, 0:n], op=ALU.mult),
                  reads=[pq[1], R_csf], writes=[t2r])
            PA.free(pq)
            P.dve(lambda e, qb=qb, t2=t2: e.tensor_tensor(out=qb[64:96, 0:n], in0=qb[64:96, 0:n], in1=t2[64:96, 0:n],
                                                          op=ALU.add), reads=[qr, t2r], writes=[qr])
            sq, sr = sqhR.get()
            P.pool(lambda e, qb=qb, sq=sq: e.tensor_tensor(out=sq[:, 0:n], in0=qb[0:96, 0:n], in1=qb[0:96, 0:n],
                                                           op=ALU.mult), reads=[qr], writes=[sr])
            qst[h] = (qb, qr, sq, sr)

        def q_stage2(h):
            qb, qr, sq, sr = qst.pop(h)
            pss = PB.get()
            mm(pss, 128, 0, n, ones[0:96, :], sq[:, 0:n], True, True, [sr, R_const], pss[1])
            rb, rr = rstd_from(pss, 96, n, 96)
            PB.free(pss)
            P.dve(lambda e, qb=qb, rb=rb, h=h: e.scalar_tensor_tensor(
                out=qT[0:96, h, 0:n], in0=qb[0:96, 0:n], scalar=gq[:, 0:1], in1=rb[0:96, 0:n],
                op0=ALU.mult, op1=ALU.mult), reads=[qr, rr, R_const], writes=[R_qT[h]])

        k_stage1, k_stage2 = make_k_stages(n, PB)
        for g in range(8):
            for h in (2 * g, 2 * g + 1):
                q_stage1(h)
                k_stage1(h)
                if h >= 2:
                    q_stage2(h - 2)
                    k_stage2(h - 2)
            conv_chunk(g)
        for h in (NH - 2, NH - 1):
            q_stage2(h)
            k_stage2(h)
        v_part(n, ti.vblk)

    LAG = 3

    def attention(ti):
        b, i = ti.b, ti.i
        KVS.limit = KVS.n + 2 * i * NH
        for h in range(NH):
            po = PB.get()
            blocks = [("meta", kT_meta[:, h, :], v_meta[:, h, :], NMETA, 0, None, [R_kTm], [R_vm], None)]
            for j in range(i):
                for t4 in range(4):
                    blocks.append(("prev", j, t4, 128, 0, None, None, None, None))
            for r in range(4):
                blocks.append(("diag", kTn[0:96, h, r * 128:(r + 1) * 128], vn[:, h, r, :], 128, r * 128, r,
                               [R_kT[h]], [R_vn], None))
            cur = {}

            def resolve(blk, h=h):
                if blk[0] != "prev":
                    return blk
                j, t4 = blk[1], blk[2]
                if t4 == 0:
                    kslot, kres, kn = KVS.get(lambda slot, j=j: [(slot[0:96, 0:T], kS[b, j, h], {})],
                                              [sres(("kS", b, j))])
                    vslot, vres, vn_ = KVS.get(lambda slot, j=j: [(slot[:, 0:T], vS[b, j, h], {})],
                                               [sres(("vS", b, j))])
                    cur["u"] = (kslot, kres, kn, vslot, vres, vn_)
                kslot, kres, kn, vslot, vres, vn_ = cur["u"]
                vv = vslot[:, 0:T].rearrange("p (t c) -> p t c", t=4)
                return ("prev", kslot[0:96, t4 * 128:(t4 + 1) * 128], vv[:, t4, :], 128, 0, None,
                        [kres], [vres], (kn, vn_) if t4 == 3 else None)
            nb = len(blocks)
            pend = []

            def do_pv(idx, blk, pt, ptr):
                kind, kap, vap, K, c0, r, kr, vr, rel = blk
                first = idx == 0
                last = idx == nb - 1
                if kind != "diag":
                    mm(po, 128, c0, T, vap[0:K, :], pt[0:K, c0:T], first, last, vr + [ptr], po[1])
                else:
                    mm(po, 128, c0, c0 + 64, vap[0:64, :], pt[0:64, c0:c0 + 64], False, False, vr + [ptr], po[1])
                    mm(po, 128, c0 + 64, T, vap[0:128, :], pt[0:128, c0 + 64:T], False, last, vr + [ptr], po[1])
                if rel is not None:
                    KVS.done(rel[0])
                    KVS.done(rel[1])

            for idx, blk in enumerate(blocks):
                blk = resolve(blk)
                kind, kap, vap, K, c0, r, kr, vr, rel = blk
                ps = PA.get()
                mm(ps, K, c0, T, kap, qT[0:96, h, c0:T], True, True, kr + [R_qT[h]], ps[1])
                pt, ptr = pTR.get()
                P.act(lambda e, ps=ps, pt=pt, K=K, c0=c0: e.activation(out=pt[0:K, c0:T], in_=ps[0][0:K, c0:T], func=AF.Exp),
                      reads=[ps[1]], writes=[ptr])
                PA.free(ps)
                pend.append((idx, blk, pt, ptr))
                if len(pend) > LAG:
                    do_pv(*pend.pop(0))
            while pend:
                do_pv(*pend.pop(0))
            rb, rr = tmpR.get()
            c = h // 2
            if h % 2 == 0:
                P.dve(lambda e, rb=rb, po=po: e.reciprocal(out=rb[0:64, :], in_=po[0][64:128, :]), reads=[po[1]], writes=[rr])
                P.dve(lambda e, rb=rb, po=po, c=c: e.tensor_tensor(out=attnT[0:64, c, :], in0=po[0][0:64, :],
                                                                   in1=rb[0:64, :], op=ALU.mult),
                      reads=[po[1], rr], writes=[R_attn[c]])
            else:
                P.dve(lambda e, rb=rb, po=po: e.reciprocal(out=rb[64:128, :], in_=po[0][0:64, :]), reads=[po[1]], writes=[rr])
                P.dve(lambda e, rb=rb, po=po, c=c: e.tensor_tensor(out=attnT[64:128, c, :], in0=po[0][64:128, :],
                                                                   in1=rb[64:128, :], op=ALU.mult),
                      reads=[po[1], rr], writes=[R_attn[c]])
            PB.free(po)

    def merge(ti):
        xt, R_xt = X["t"], X["R"]
        n = ti.n
        for m in range(8):
            u1, r1, n1 = wunit(wco_s[m], [("wco", m)])
            u2, r2, n2 = wunit(win_s[U_GC + m], [("win", U_GC + m)])
            u3, r3, n3 = wunit(wmo_s[m], [("wmo", m)])
            u4, r4, n4 = wunit(win_s[U_GM + m], [("win", U_GM + m)])
            pcb = PA.get()
            pgc = PA.get()
            pmb = PA.get()
            pgm = PA.get()
            for (u, ru, pp, nn, rhs, rres) in ((u1, r1, pcb, n1, gated, R_gated), (u2, r2, pgc, n2, hT, R_hT),
                                                (u3, r3, pmb, n3, attnT, R_attn), (u4, r4, pgm, n4, hT, R_hT)):
                uvw = u[:].rearrange("p (k m) -> p k m", k=8)
                for kc in range(8):
                    mm(pp, 128, 0, n, uvw[:, kc, :], rhs[:, kc, 0:n], kc == 0, kc == 7, [ru, rres[kc]], pp[1])
                WS.done(nn)
            s1, s1r = tmpR.get()
            s2, s2r = tmpR.get()
            P.act(lambda e, s1=s1, pgc=pgc: e.activation(out=s1[:, 0:n], in_=pgc[0][:, 0:n], func=AF.Sigmoid),
                  reads=[pgc[1]], writes=[s1r])
            P.act(lambda e, s2=s2, pgm=pgm: e.activation(out=s2[:, 0:n], in_=pgm[0][:, 0:n], func=AF.Sigmoid),
                  reads=[pgm[1]], writes=[s2r])
            PA.free(pgc)
            PA.free(pgm)
            P.dve(lambda e, s1=s1, pcb=pcb: e.tensor_tensor(out=s1[:, 0:n], in0=pcb[0][:, 0:n], in1=s1[:, 0:n], op=ALU.mult),
                  reads=[pcb[1], s1r], writes=[s1r])
            P.dve(lambda e, s2=s2, pmb=pmb: e.tensor_tensor(out=s2[:, 0:n], in0=pmb[0][:, 0:n], in1=s2[:, 0:n], op=ALU.mult),
                  reads=[pmb[1], s2r], writes=[s2r])
            PA.free(pcb)
            PA.free(pmb)
            P.pool(lambda e, s1=s1, s2=s2, m=m: e.tensor_tensor(out=mergedT[:, m, 0:n], in0=s1[:, 0:n], in1=s2[:, 0:n],
                                                                op=ALU.add), reads=[s1r, s2r], writes=[R_merged[m]])
        for half in range(2):
            pd = [PA.get() for _ in ti.subs]
            for kk in range(4):
                src = woa_s[kk * 256:(kk + 1) * 256, half * T:(half + 1) * T].rearrange("(t p) n -> p t n", p=128)
                u, ru, nu = wunit(src, [("woa", 2 * kk), ("woa", 2 * kk + 1)])
                uv = u[:].rearrange("p (t n) -> p t n", t=2)
                for t2 in range(2):
                    kc = 2 * kk + t2
                    for s, nr in enumerate(ti.subs):
                        mm(pd[s], nr, 0, T, mergedT[:, kc, s * 128:s * 128 + nr], uv[:, t2, :], kc == 0, kc == 7,
                           [ru, R_merged[kc]], pd[s][1])
                WS.done(nu)
            for s, nr in enumerate(ti.subs):
                P.dve(lambda e, s=s, nr=nr, pd=pd, half=half: e.tensor_tensor(
                    out=xt[0:nr, s, half * T:(half + 1) * T], in0=pd[s][0][0:nr, :],
                    in1=xt[0:nr, s, half * T:(half + 1) * T], op=ALU.add),
                    reads=[pd[s][1], R_xt[s][half]], writes=[R_xt[s][half]])
                PA.free(pd[s])

    def sample_attention():
        pos = [PB.get(), PB.get()]

        def block_pass(sbi, ktile_aps, first, last_block):
            q0 = NMETA + DEC * sbi
            nbk = len(ktile_aps)
            for bi, (kf, vf, K, krf, vres) in enumerate(ktile_aps):
                ps = PA.get()
                for h in range(NH):
                    mm(ps, K, h * DEC, (h + 1) * DEC, kf(h), qT[0:96, h, q0:q0 + DEC], True, True,
                       [krf(h), R_qT[h]], ps[1])
                pt, ptr = pTR.get()
                P.act(lambda e, ps=ps, pt=pt, K=K: e.activation(out=pt[0:K, :], in_=ps[0][0:K, :], func=AF.Exp),
                      reads=[ps[1]], writes=[ptr])
                PA.free(ps)
                for h in range(NH):
                    mm(pos[sbi], 128, h * DEC, (h + 1) * DEC, vf(h), pt[0:K, h * DEC:(h + 1) * DEC],
                       first and bi == 0 and h == 0, last_block and bi == nbk - 1 and h == NH - 1,
                       [vres, ptr], pos[sbi][1])

        for sbi in range(2):
            q0 = NMETA + DEC * sbi
            block_pass(sbi, [(lambda h, q0=q0: kTn[0:96, h, q0:q0 + DEC],
                              lambda h, sbi=sbi: vn[0:DEC, h, 1 + sbi, :], DEC, lambda h: R_kT[h], R_vn)], True, False)
        for sbi in range(2):
            for cb in range(PAST // T):
                r0 = cb * T
                P.dma("pool", ch_io, stage[:, :, 0:128], c_kv[sbi, r0:r0 + T, :].rearrange("(s p) c -> p s c", p=128),
                      writes=[R_stage])
                P.dma("pool", ch_io, stage[:, :, 128:160], c_kr[sbi, r0:r0 + T, :].rearrange("(s p) c -> p s c", p=128),
                      writes=[R_stage])
                P.pool(lambda e: e.tensor_copy(out=kvb[:], in_=stage[:]), reads=[R_stage], writes=[R_kvb])
                kv_transpose([128] * 4)
                kv_prep(T, [(s * 128, 128) for s in range(4)], PA)
                block_pass(sbi, [(lambda h, t4=t4: kTn[0:96, h, t4 * 128:(t4 + 1) * 128],
                                  lambda h, t4=t4: vn[:, h, t4, :], 128, lambda h: R_kT[h], R_vn) for t4 in range(4)],
                           False, cb == PAST // T - 1)
        for sbi in range(2):
            q0 = NMETA + DEC * sbi
            po = pos[sbi]
            pv4 = po[0][:].rearrange("p (a b q) -> p a b q", a=8, b=2)
            rb, rr = tmpR.get()
            rb4 = rb[:, 0:256].rearrange("p (a q) -> p a q", a=8)
            P.dve(lambda e, rb4=rb4, pv4=pv4: e.reciprocal(out=rb4[0:64], in_=pv4[64:128, :, 0, :]), reads=[po[1]], writes=[rr])
            P.dve(lambda e, rb4=rb4, pv4=pv4: e.reciprocal(out=rb4[64:128], in_=pv4[0:64, :, 1, :]), reads=[po[1]], writes=[rr])
            P.dve(lambda e, rb4=rb4, pv4=pv4, q0=q0: e.tensor_tensor(out=attnT[0:64, :, q0:q0 + DEC], in0=pv4[0:64, :, 0, :],
                                                                     in1=rb4[0:64], op=ALU.mult),
                  reads=[po[1], rr], writes=R_attn)
            P.dve(lambda e, rb4=rb4, pv4=pv4, q0=q0: e.tensor_tensor(out=attnT[64:128, :, q0:q0 + DEC], in0=pv4[64:128, :, 1, :],
                                                                     in1=rb4[64:128], op=ALU.mult),
                  reads=[po[1], rr], writes=R_attn)
            PB.free(po)

    def axt(k):
        return [xbufs[k % 2][1][s_][h_] for s_ in range(4) for h_ in range(2)]

    tiles = [make_tile("spec")] + [make_tile("reg", b, i) for b in range(2) for i in range(NT)]
    if stg < 99:
        tiles = tiles[:max(1, stg)]

    def load_x(k):
        ti = tiles[k]
        xt_k = xbufs[k % 2][0]
        if ti.kind == "spec":
            P.dma("pool", ch_io, xt_k[0:NMETA, 0, :], meta, writes=axt(k))
            for sbi in range(2):
                r0 = NMETA + DEC * sbi
                P.dma("pool", ch_io, xt_k[r0:r0 + DEC, 0, :], x_s[sbi], writes=axt(k))
        else:
            P.dma("pool", ch_io, xt_k[:], x_p[ti.b, ti.i * T:(ti.i + 1) * T, :].rearrange("(s p) d -> p s d", p=128),
                  writes=axt(k))

    def use(k):
        X["t"], X["R"] = xbufs[k % 2]

    load_x(0)
    cast_ffn(0)
    for u in range(NWIN):
        cast(win_s[u], win_in[u], ("win", u))
    for m in range(8):
        cast(wco_s[m], wco_in[m], ("wco", m))
        cast(wmo_s[m], wmo_in[m], ("wmo", m))
    for k in range(8):
        cast(woa_s[k * 128:(k + 1) * 128, :], woa_in[k * 128:(k + 1) * 128, :], ("woa", k))
    cast_ffn(1)

    use(0)
    rms_to_hT(tiles[0], 0)
    for k, ti in enumerate(tiles):
        use(k)
        ffn(ti, 0, pre=True)
        if k + 1 < len(tiles):
            load_x(k + 1)
        in_all(ti)
        if ti.kind == "spec":
            P.pool(lambda e: e.tensor_copy(out=kT_meta[:], in_=kTn[0:96, :, 0:NMETA]), reads=R_kT, writes=[R_kTm])
            P.pool(lambda e: e.tensor_copy(out=v_meta[:], in_=vn[0:NMETA, :, 0, :]), reads=[R_vn], writes=[R_vm])
            for sbi in range(2):
                for r in range(2):
                    P.dma("pool", ch_io, nconv_s[sbi, r, :].rearrange("(c p) -> p c", p=128), ncs[:, :, sbi, r],
                          reads=[R_ncs], allow_slow_non_contiguous=True)
            sample_attention()
        else:
            b, i = ti.b, ti.i
            if i < NT - 1:
                P.dma("pool", ch_io, kS[b, i].rearrange("h p n -> p h n"), kTn[0:96, :, :], reads=R_kT,
                      writes=[sres(("kS", b, i))])
                P.dma("pool", ch_io, vS[b, i].rearrange("h p n -> p h n"),
                      vn[:].rearrange("p h s c -> p h (s c)"), reads=[R_vn], writes=[sres(("vS", b, i))])
            attention(ti)
        merge(ti)

        def hook(k=k):
            if k + 1 < len(tiles):
                use(k + 1)
                rms_to_hT(tiles[k + 1], 0)
                use(k)

        ffn(ti, 1, mid_hook=hook)
        xt_k = xbufs[k % 2][0]
        if ti.kind == "spec":
            for sbi in range(2):
                r0 = NMETA + DEC * sbi
                P.dma("pool", ch_io, y_s[sbi], xt_k[r0:r0 + DEC, 0, :], reads=axt(k))
        else:
            P.dma("pool", ch_io, y_p[ti.b, ti.i * T:(ti.i + 1) * T, :].rearrange("(s p) d -> p s d", p=128), xt_k[:],
                  reads=axt(k))
            if ti.i == NT - 1:
                for r in range(2):
                    P.dma("pool", ch_io, nconv_p[ti.b, r, :].rearrange("(c p) -> p c", p=128), halo[:, :, r],
                          reads=[R_halo], allow_slow_non_contiguous=True)

    P.emit()
    st.close()
    return nc


def _unit_image(w, c0, ncols=128):
    blk = w[:, c0:c0 + ncols].reshape(8, 128, ncols)
    return np.ascontiguousarray(blk.transpose(1, 0, 2).reshape(128, 8 * ncols))


def _rope_tables():
    half = 16
    inv_freq = (np.float32(10000.0) ** (-np.arange(half, dtype=np.float32) / np.float32(half))).astype(np.float32)
    pos = np.concatenate([np.arange(NMETA), NMETA + PAST + np.arange(DEC), NMETA + PAST + np.arange(DEC),
                          NMETA + np.arange(SEQ)]).astype(np.float32)
    ang = pos[:, None] * inv_freq[None, :]
    cos = np.cos(ang).astype(np.float32)
    sin = np.sin(ang).astype(np.float32)
    cs_tm = np.concatenate([cos, cos, sin, sin], axis=1)
    cs_fm = np.zeros((128, cos.shape[0]), np.float32)
    cs_fm[64:96] = np.concatenate([cos, cos], 1).T
    cs_fm[96:128] = np.concatenate([sin, sin], 1).T
    return np.ascontiguousarray(cs_fm), np.ascontiguousarray(cs_tm)


_NC_CACHE = {}


def kernel(x_prompt, x_sample, cache_conv, cache_kv_latent, cache_k_rope, meta_tokens,
           ffn1_norm, ffn1_w_gate, ffn1_w_up, ffn1_w_down, mix_norm, w_in_all, conv_w,
           w_conv_out, q_a_norm, w_uq, kv_a_norm, w_ukv, q_norm, k_norm, w_mla_out,
           w_out_all, ffn2_norm, ffn2_w_gate, ffn2_w_up, ffn2_w_down):
    f = lambda a: np.asarray(a, dtype=np.float32)
    x_prompt, x_sample = f(x_prompt), f(x_sample)
    cache_conv, cache_kv_latent, cache_k_rope = f(cache_conv)[0], f(cache_kv_latent)[0], f(cache_k_rope)[0]
    win = f(w_in_all)[0]
    shared = {}
    for nm, wg, wu, wd in (("1", ffn1_w_gate, ffn1_w_up, ffn1_w_down), ("2", ffn2_w_gate, ffn2_w_up, ffn2_w_down)):
        wg, wu, wd = f(wg)[0], f(wu)[0], f(wd)[0]
        shared["wg" + nm] = np.stack([_unit_image(wg, j * 128) for j in range(NJ)])
        shared["wu" + nm] = np.stack([_unit_image(wu, j * 128) for j in range(NJ)])
        shared["wd" + nm] = np.ascontiguousarray(wd)
    offs = {"b": 0, "c": 1024, "v": 2048, "q": 3072, "kv": 3456, "gc": 3616, "gm": 4640}
    units = [offs["c"] + 128 * j for j in range(8)] + [offs["v"] + 128 * j for j in range(8)] + \
            [offs["b"] + 128 * j for j in range(8)] + [offs["q"] + 128 * j for j in range(3)] + \
            [offs["gc"] + 128 * j for j in range(8)] + [offs["gm"] + 128 * j for j in range(8)]
    shared["win_u"] = np.stack([_unit_image(win, c0) for c0 in units])
    shared["wkv"] = _unit_image(win, offs["kv"], 160)
    shared["wco_u"] = np.stack([_unit_image(f(w_conv_out)[0], m * 128) for m in range(8)])
    shared["wmo_u"] = np.stack([_unit_image(f(w_mla_out)[0], m * 128) for m in range(8)])
    shared["woa"] = np.ascontiguousarray(f(w_out_all)[0])
    wuq = f(w_uq)[0].reshape(3, 128, NH, 96)
    wq_c = np.concatenate([wuq, wuq[..., 80:96], wuq[..., 64:80]], axis=-1)
    shared["wq_p"] = np.ascontiguousarray(wq_c.transpose(1, 0, 2, 3).reshape(128, -1))
    wukv = f(w_ukv)[0].reshape(128, NH, 128)
    wka = np.concatenate([wukv[:, :, 0:64], np.zeros((128, NH, 64), np.float32)], axis=-1)
    shared["wka"] = np.ascontiguousarray(wka.reshape(128, -1))
    shared["wuv"] = np.ascontiguousarray(wukv[:, :, 64:128].reshape(128, -1))
    bsel = np.zeros((32, 128), np.float32)
    bsel[np.arange(32), 64 + np.arange(32)] = 1.0
    shared["bsel"] = bsel
    shared["ident"] = np.eye(128, dtype=np.float32)
    gT = np.concatenate([f(g)[0].reshape(8, 128).T for g in (ffn1_norm, mix_norm, ffn2_norm)], axis=1)
    shared["gT"] = np.ascontiguousarray(gT)
    shared["gqaT"] = np.ascontiguousarray(f(q_a_norm)[0].reshape(3, 128).T)
    shared["gkv_bc"] = np.ascontiguousarray(np.broadcast_to(f(kv_a_norm)[0][None, :], (128, 128)))
    perm = np.concatenate([np.arange(64, 96), np.arange(0, 64)])
    shared["gq_p"] = np.ascontiguousarray(f(q_norm)[0][:, None])
    shared["gk_p"] = np.ascontiguousarray(f(k_norm)[0][:, None])
    cw = f(conv_w)[0]
    shared["cwT"] = np.ascontiguousarray(cw.reshape(3, 8, 128).transpose(2, 1, 0).reshape(128, 24))
    cs_fm, cs_tm = _rope_tables()
    shared["cs_fm"] = cs_fm
    shared["cs_tm"] = cs_tm
    shared["meta"] = np.ascontiguousarray(f(meta_tokens))

    if "nc" not in _NC_CACHE:
        _NC_CACHE["nc"] = build_program()
    nc = _NC_CACHE["nc"]
    in_maps = []
    for c in range(NCORES):
        m = dict(shared)
        m["x_p"] = np.ascontiguousarray(x_prompt[2 * c:2 * c + 2])
        m["x_s"] = np.ascontiguousarray(x_sample[2 * c:2 * c + 2])
        m["c_conv"] = np.ascontiguousarray(cache_conv[2 * c:2 * c + 2])
        m["c_kv"] = np.ascontiguousarray(cache_kv_latent[2 * c:2 * c + 2])
        m["c_kr"] = np.ascontiguousarray(cache_k_rope[2 * c:2 * c + 2])
        in_maps.append(m)
    res = run_bass_kernel_spmd(nc, in_maps, core_ids=list(range(NCORES)))
    R = res.results
    cat = lambda k: np.concatenate([np.asarray(r[k], dtype=np.float32) for r in R], axis=0)
    return (cat("y_p"), cat("y_s"), cat("nconv_p")[None], cat("nkv_p")[None], cat("nkr_p")[None],
            cat("nconv_s")[None], cat("nkv_s")[None], cat("nkr_s")[None])
```

```python
import numpy as np
import concourse.bass as bass
import concourse.mybir as mybir
from concourse.bass_utils import run_bass_kernel_spmd

F32 = mybir.dt.float32
BF16 = mybir.dt.bfloat16
AF = mybir.ActivationFunctionType
ALU = mybir.AluOpType

NCORES = 8
D = 1024
FF = 2816
NJ = FF // 128
SEQ = 4096
NMETA = 16
LP = SEQ + NMETA
DEC = 32
PAST = 2048
T = 512
NT = SEQ // T
NH = 16
QK = 96
EPS = 1e-6
NSPEC = NMETA + 2 * DEC
NCOL = NSPEC + SEQ
DBG = {}

U_C, U_V, U_B, U_Q, U_GC, U_GM = 0, 8, 16, 24, 27, 35
NWIN = 43


class Res:
    __slots__ = ("w", "rs", "name")

    def __init__(self, name=""):
        self.w = None
        self.rs = []
        self.name = name


class Op:
    __slots__ = ("eng", "fn", "deps", "inc", "val", "is_dma", "chan", "ci", "dead", "idx")


class Chan:
    def __init__(self, name, n):
        self.name = name
        self.n = n
        self.k = 0
        self.count = [0] * n
        self.last = [None] * n
        self.sems = [None] * n


ENGS = ("pe", "act", "dve", "pool", "sp")


class Prog:
    def __init__(self, nc):
        self.nc = nc
        self.lists = {e: [] for e in ENGS}
        self.chans = []

    def chan(self, name, n):
        c = Chan(name, n)
        self.chans.append(c)
        return c

    def op(self, eng, fn, reads=(), writes=(), chan=None):
        o = Op()
        o.eng = eng
        o.fn = fn
        o.inc = False
        o.val = 0
        o.is_dma = chan is not None
        o.chan = chan
        o.ci = 0
        o.dead = False
        raw = []
        oth = []
        for r in reads:
            if r.w is not None:
                raw.append(r.w)
        for w in writes:
            if w.w is not None:
                oth.append(w.w)
            oth.extend(w.rs)
        deps = {}
        for d in raw:
            if d is o:
                continue
            if (not d.is_dma) and (not o.is_dma) and d.eng == eng and eng == "pe":
                continue
            deps[id(d)] = d
        for d in oth:
            if d is o:
                continue
            if (not d.is_dma) and (not o.is_dma) and d.eng == eng and eng == "pe":
                continue
            deps[id(d)] = d
        if chan is not None:
            i = chan.k % chan.n
            chan.k += 1
            if chan.last[i] is not None:
                deps[id(chan.last[i])] = chan.last[i]
            chan.count[i] += 16
            o.ci = i
            o.val = chan.count[i]
            chan.last[i] = o
        best = {}
        final = []
        for d in deps.values():
            if d.is_dma:
                final.append(d)
            elif d.eng not in best or d.idx > best[d.eng].idx:
                best[d.eng] = d
        final.extend(best.values())
        o.deps = final
        o.idx = len(self.lists[eng])
        for d in o.deps:
            d.inc = True
        for r in reads:
            r.rs.append(o)
        for w in writes:
            w.w = o
            w.rs = []
        self.lists[eng].append(o)
        return o

    def pe(self, fn, reads=(), writes=()):
        return self.op("pe", fn, reads, writes)

    def act(self, fn, reads=(), writes=()):
        return self.op("act", fn, reads, writes)

    def dve(self, fn, reads=(), writes=()):
        return self.op("dve", fn, reads, writes)

    def pool(self, fn, reads=(), writes=()):
        return self.op("pool", fn, reads, writes)

    def dma(self, eng, chan, out, in_, reads=(), writes=(), **kw):
        return self.op(eng, lambda e: e.dma_start(out=out, in_=in_, **kw), reads, writes, chan=chan)

    def check(self):
        ptr = {e: 0 for e in ENGS}
        done = set()
        lists = {e: [o for o in self.lists[e] if not o.dead] for e in ENGS}
        progress = True
        while progress:
            progress = False
            for e in ENGS:
                L = lists[e]
                while ptr[e] < len(L):
                    o = L[ptr[e]]
                    if all((d.dead or id(d) in done) for d in o.deps):
                        done.add(id(o))
                        ptr[e] += 1
                        progress = True
                    else:
                        break
        stuck = {e: (ptr[e], len(lists[e])) for e in ENGS if ptr[e] < len(lists[e])}
        assert not stuck, "dependency deadlock: %r" % (stuck,)

    def emit(self):
        self.check()
        nc = self.nc
        import contextlib

        with contextlib.ExitStack() as st:
            esem = {}
            for e in ("pe", "act", "dve", "pool"):
                esem[e] = st.enter_context(nc.semaphore("s_" + e))
            for c in self.chans:
                for i in range(c.n):
                    c.sems[i] = st.enter_context(nc.semaphore("c_%s_%d" % (c.name, i)))
            for e in ("pe", "act", "dve", "pool"):
                cnt = 0
                for o in self.lists[e]:
                    if o.is_dma or o.dead:
                        continue
                    if o.inc:
                        cnt += 1
                        o.val = cnt
            block = st.enter_context(nc.Block())

            def run(engname, eng, final=False):
                seen = {}
                for o in self.lists[engname]:
                    if o.dead:
                        continue
                    for d in o.deps:
                        if d.dead:
                            continue
                        if d.is_dma:
                            sem = d.chan.sems[d.ci]
                        else:
                            sem = esem[d.eng]
                        key = id(sem)
                        if seen.get(key, 0) < d.val:
                            eng.wait_ge(sem, d.val)
                            seen[key] = d.val
                    ins = o.fn(eng)
                    if o.is_dma:
                        ins.then_inc(o.chan.sems[o.ci], 16)
                    elif o.inc:
                        ins.then_inc(esem[engname], 1)
                if final:
                    fin = {}
                    for en in ENGS:
                        for o in self.lists[en]:
                            if o.is_dma and not o.dead:
                                k = (id(o.chan), o.ci)
                                fin[k] = max(fin.get(k, 0), o.val)
                    for c in self.chans:
                        for i in range(c.n):
                            v = fin.get((id(c), i), 0)
                            if v > 0:
                                eng.wait_ge(c.sems[i], v)

            @block.tensor
            def _(e):
                run("pe", e)

            @block.scalar
            def _(e):
                run("act", e)

            @block.vector
            def _(e):
                run("dve", e)

            @block.gpsimd
            def _(e):
                run("pool", e)

            @block.sync
            def _(e):
                run("sp", e, final=True)


class BankPool:
    def __init__(self, banks):
        self.banks = banks
        self.freeq = list(range(len(banks)))

    def get(self):
        assert self.freeq, "PSUM pool exhausted"
        i = self.freeq.pop(0)
        t, r = self.banks[i]
        return (t, r, i)

    def free(self, b):
        assert b[2] not in self.freeq
        self.freeq.append(b[2])


class Ring:
    def __init__(self, items):
        self.items = items
        self.k = 0

    def get(self):
        it = self.items[self.k % len(self.items)]
        self.k += 1
        return it


class Stream:
    def __init__(self, P, name, slots, chan):
        self.P = P
        self.slots = slots
        self.n = 0
        self.chan = chan
        self.pending = {}
        self.released = set()
        self.limit = None

    def _issue(self, n, dep_reads):
        slot_t, slot_r = self.slots[n % len(self.slots)]
        cell = {"parts": None}

        def fn(e, cell=cell):
            ins = None
            for (o, i, kw) in cell["parts"]:
                ins = e.dma_start(out=o, in_=i, **kw)
            return ins

        o = self.P.op("sp", fn, reads=dep_reads, writes=[slot_r], chan=self.chan)
        o.dead = True
        self.pending[n] = (o, cell)

    def get(self, make_parts, src_res):
        n = self.n
        self.n += 1
        if n not in self.pending:
            assert n < len(self.slots) or (n - len(self.slots)) in self.released, "stream ring too small"
            self._issue(n, [])
        o, cell = self.pending.pop(n)
        slot_t, slot_r = self.slots[n % len(self.slots)]
        parts = make_parts(slot_t)
        assert len(parts) == 1
        cell["parts"] = parts
        o.dead = False
        for r in src_res:
            if r.w is not None and all(d is not r.w for d in o.deps):
                o.deps.append(r.w)
                r.w.inc = True
        return slot_t, slot_r, n

    def done(self, n):
        self.released.add(n)
        nxt = n + len(self.slots)
        if self.limit is None or nxt < self.limit:
            self._issue(nxt, [])


def build_program(stg=99):
    nc = bass.Bass("TRN2", target_bir_lowering=False)
    P = Prog(nc)
    import contextlib

    st = contextlib.ExitStack()

    def din(name, shape, dt=F32):
        return nc.dram_tensor(name, list(shape), dt, kind="ExternalInput").ap()

    def dout(name, shape):
        return nc.dram_tensor(name, list(shape), F32, kind="ExternalOutput").ap()

    def dscr(name, shape, dt=BF16):
        return nc.dram_tensor(name, list(shape), dt, kind="Internal").ap()

    x_p = din("x_p", [2, SEQ, D])
    x_s = din("x_s", [2, DEC, D])
    c_conv = din("c_conv", [2, 2, D])
    c_kv = din("c_kv", [2, PAST, 128])
    c_kr = din("c_kr", [2, PAST, 32])
    meta = din("meta", [NMETA, D])
    wg_in = [din("wg1", [NJ, 128, 1024]), din("wg2", [NJ, 128, 1024])]
    wu_in = [din("wu1", [NJ, 128, 1024]), din("wu2", [NJ, 128, 1024])]
    wd_in = [din("wd1", [FF, D]), din("wd2", [FF, D])]
    win_in = din("win_u", [NWIN, 128, 1024])
    wkv_in = din("wkv", [128, 8 * 160])
    wco_in = din("wco_u", [8, 128, 1024])
    wmo_in = din("wmo_u", [8, 128, 1024])
    woa_in = din("woa", [D, D])
    wq_in = din("wq_p", [128, 3 * NH * 128])
    wka_in = din("wka", [128, NH * 128])
    wuv_in = din("wuv", [128, 1024])
    bsel_in = din("bsel", [32, 128])
    ident_in = din("ident", [128, 128])
    gT_in = din("gT", [128, 24])
    gqa_in = din("gqaT", [128, 3])
    gkv_in = din("gkv_bc", [128, 128])
    gq_in = din("gq_p", [96, 1])
    gk_in = din("gk_p", [96, 1])
    cw_in = din("cwT", [128, 24])
    csfm_in = din("cs_fm", [128, NCOL])
    cstm_in = din("cs_tm", [NCOL, 64])

    y_p = dout("y_p", [2, SEQ, D])
    y_s = dout("y_s", [2, DEC, D])
    nconv_p = dout("nconv_p", [2, 2, D])
    nkv_p = dout("nkv_p", [2, LP, 128])
    nkr_p = dout("nkr_p", [2, LP, 32])
    nconv_s = dout("nconv_s", [2, 2, D])
    nkv_s = dout("nkv_s", [2, DEC, 128])
    nkr_s = dout("nkr_s", [2, DEC, 32])

    wg_s = [dscr("wg1_s", [NJ, 128, 1024]), dscr("wg2_s", [NJ, 128, 1024])]
    wu_s = [dscr("wu1_s", [NJ, 128, 1024]), dscr("wu2_s", [NJ, 128, 1024])]
    wd_s = [dscr("wd1_s", [FF, D]), dscr("wd2_s", [FF, D])]
    win_s = dscr("win_s", [NWIN, 128, 1024])
    wco_s = dscr("wco_s", [8, 128, 1024])
    wmo_s = dscr("wmo_s", [8, 128, 1024])
    woa_s = dscr("woa_s", [D, D])
    kS = dscr("kS", [2, NT, NH, 96, T])
    vS = dscr("vS", [2, NT, NH, 128, T])

    def sb(name, shape, dt=F32):
        return st.enter_context(nc.sbuf_tensor("S_" + name, list(shape), dt))

    NSLOT = 8
    wslots = [(sb("wsl%d" % i, [128, 1024], BF16), Res("wsl%d" % i)) for i in range(NSLOT)]
    NKV = 5
    kvslots = [(sb("kvsl%d" % i, [128, T], BF16), Res("kvsl%d" % i)) for i in range(NKV)]
    xbufs = []
    for xi in range(2):
        xbufs.append((sb("xt%d" % xi, [128, 4, D]),
                      [[Res("xt%d_%d_%d" % (xi, s, h)) for h in range(2)] for s in range(4)]))
    X = {"t": xbufs[0][0], "R": xbufs[0][1]}
    xn = [(sb("xn%d" % i, [128, D], BF16), Res("xn%d" % i)) for i in range(2)]
    xnR = Ring(xn)
    R_junk = Res("junk")
    hT = sb("hT", [128, 8, T], BF16)
    R_hT = [Res("hT%d" % k) for k in range(8)]
    arena = sb("arena", [128, 2 * NH * T], BF16)
    aT = arena[:, 0:NJ * T].rearrange("p (j t) -> p j t", t=T)
    qT = arena[:, 0:NH * T].rearrange("p (h t) -> p h t", t=T)
    kTn = arena[:, NH * T:2 * NH * T].rearrange("p (h t) -> p h t", t=T)
    R_aT = [Res("aT%d" % j) for j in range(NJ)]
    R_qT = [Res("qT%d" % h) for h in range(NH)]
    R_kT = [Res("kT%d" % h) for h in range(NH)]
    vn = sb("vn", [128, NH, 4, 128], BF16)
    R_vn = Res("vn")
    gated = sb("gated", [128, 8, T], BF16)
    R_gated = [Res("gated%d" % k) for k in range(8)]
    attnT = sb("attnT", [128, 8, T], BF16)
    R_attn = [Res("attn%d" % k) for k in range(8)]
    mergedT = arena[:, 0:8 * T].rearrange("p (k t) -> p k t", t=T)
    R_merged = [Res("merged%d" % k) for k in range(8)]
    sqb = sb("sqb", [128, 3, T], BF16)
    R_sqb = Res("sqb")
    junk = sqb[:].rearrange("p a b -> p (a b)")
    qlT = sb("qlT", [128, 3, T], BF16)
    R_qlT = Res("qlT")
    ckvT = sb("ckvT", [128, T], BF16)
    R_ckvT = Res("ckvT")
    kpeT = sb("kpeT", [32, T], BF16)
    R_kpeT = Res("kpeT")
    kvt = sb("kvt", [128, 4, 160])
    R_kvt = Res("kvt")
    ckv_tm = sb("ckv_tm", [128, 4, 128])
    R_ckvtm = Res("ckv_tm")
    kpe_tm = sb("kpe_tm", [128, 4, 32])
    R_kpetm = Res("kpe_tm")
    rt1 = sb("rt1", [128, 4, 32])
    rt2 = sb("rt2", [128, 4, 32])
    R_rt = Res("rt")
    kvb = sb("kvb", [128, 4, 160], BF16)
    R_kvb = Res("kvb")
    tmpf = [(sb("tmpf%d" % i, [128, T]), Res("tmpf%d" % i)) for i in range(5)]
    tmpR = Ring(tmpf)
    qbf = [(sb("qbf%d" % i, [128, T]), Res("qbf%d" % i)) for i in range(4)]
    qbR = Ring(qbf)
    sqh = [(sb("sqh%d" % i, [96, T], BF16), Res("sqh%d" % i)) for i in range(6)]
    sqhR = Ring(sqh)
    pTb = [(sb("pT%d" % i, [128, T], BF16), Res("pT%d" % i)) for i in range(5)]
    pTR = Ring(pTb)
    cinb = [(sb("cin%d" % i, [128, T + 8]), Res("cin%d" % i)) for i in range(2)]
    cinR = Ring(cinb)
    halo = sb("halo", [128, 8, 2])
    R_halo = Res("halo")
    halo_meta = sb("halo_meta", [128, 8, 2])
    R_halo_meta = Res("halo_meta")
    cch = sb("cch", [128, 8, 2, 2])
    R_cch = Res("cch")
    ncs = sb("ncs", [128, 8, 2, 2])
    R_ncs = Res("ncs")
    ss4 = sb("ss4", [128, 8])
    R_ss4 = Res("ss4")
    kT_meta = sb("kT_meta", [96, NH, NMETA], BF16)
    R_kTm = Res("kT_meta")
    v_meta = sb("v_meta", [NMETA, NH, 128], BF16)
    R_vm = Res("v_meta")
    cs_f = sb("cs_f", [128, T])
    R_csf = Res("cs_f")
    cs_t = sb("cs_t", [128, 4, 64])
    R_cst = Res("cs_t")
    stage = kvt
    R_stage = R_kvt
    wq = sb("wq", [128, 3, NH, 128], BF16)
    wka = sb("wka", [128, NH, 128], BF16)
    wuv = sb("wuv", [128, 1024], BF16)
    wkv = sb("wkv", [128, 8, 160], BF16)
    bsel = sb("bsel", [32, 128], BF16)
    ident = sb("ident", [128, 128], BF16)
    ones = sb("ones", [128, 128], BF16)
    mhalf = sb("mhalf", [128, 4])
    gT = sb("gT", [128, 24])
    gqa = sb("gqa", [128, 3])
    gkv = sb("gkv", [128, 128])
    gq = sb("gq", [96, 1])
    gk = sb("gk", [96, 1])
    cw = sb("cw", [128, 8, 3])
    R_const = Res("const")

    psum = [(st.enter_context(nc.psum_tensor("ps%d" % i, [128, T], F32)), Res("ps%d" % i)) for i in range(8)]
    PA = BankPool(psum[0:6])
    PB = BankPool(psum[6:8])

    ch_w = P.chan("w", NSLOT)
    ch_kv = P.chan("kv", NKV)
    ch_cast = P.chan("cast", 48)
    ch_io = P.chan("io", 8)
    ch_c = P.chan("const", 8)

    WS = Stream(P, "w", wslots, ch_w)
    KVS = Stream(P, "kv", kvslots, ch_kv)

    scr_res = {}

    def sres(key):
        if key not in scr_res:
            scr_res[key] = Res(str(key))
        return scr_res[key]

    def cload(dst, src, cast=False):
        P.dma("pool", ch_c, dst, src, writes=[R_const])

    cload(gT[:], gT_in)
    cload(gqa[:], gqa_in)
    cload(gkv[:], gkv_in)
    cload(gq[:], gq_in)
    cload(gk[:], gk_in)
    cload(cw[:].rearrange("p a b -> p (a b)"), cw_in)
    cload(ident[:], ident_in)
    cload(bsel[:], bsel_in)
    cload(wq[:].rearrange("p a b c -> p (a b c)"), wq_in)
    cload(wka[:].rearrange("p a b -> p (a b)"), wka_in)
    cload(wuv[:], wuv_in)
    cload(wkv[:].rearrange("p a b -> p (a b)"), wkv_in)
    with nc.allow_non_contiguous_dma(reason="tiny transposed cache_conv load"):
        for sbi in range(2):
            for r in range(2):
                P.dma("pool", ch_c, cch[:, :, sbi, r],
                      c_conv[sbi, r, :].rearrange("(c p) -> p c", p=128), writes=[R_const],
                      allow_slow_non_contiguous=True)
    P.pool(lambda e: e.memset(ones[:], 1.0), writes=[R_const])
    P.pool(lambda e: e.memset(mhalf[:], -0.5), writes=[R_const])
    P.pool(lambda e: e.memset(vn[:].rearrange("p a b c -> p (a b c)"), 1.0), writes=[R_vn])
    P.pool(lambda e: e.memset(v_meta[:].rearrange("p a b -> p (a b)"), 1.0), writes=[R_vm])
    P.pool(lambda e: e.memset(attnT[:].rearrange("p a b -> p (a b)"), 0.0), writes=R_attn)
    P.pool(lambda e: e.memset(halo_meta[:].rearrange("p a b -> p (a b)"), 0.0), writes=[R_halo_meta])
    for c3 in range(3):
        P.dve(lambda e, c3=c3: e.tensor_scalar(out=wq[:, c3, :, 96:112], in0=wq[:, c3, :, 96:112], scalar1=-1.0,
                                               scalar2=None, op0=ALU.mult), reads=[R_const], writes=[R_const])
    P.dve(lambda e: e.tensor_scalar(out=gq[:], in0=gq[:], scalar1=float(QK ** -0.5), scalar2=None,
                                    op0=ALU.mult), reads=[R_const], writes=[R_const])

    def cast(dst, src, key):
        P.dma("pool", ch_cast, dst, src, writes=[sres(key)])

    def cast_ffn(f):
        for j in range(NJ):
            cast(wg_s[f][j], wg_in[f][j], ("wg", f, j))
            cast(wu_s[f][j], wu_in[f][j], ("wu", f, j))
        for j in range(NJ):
            cast(wd_s[f][j * 128:(j + 1) * 128, :], wd_in[f][j * 128:(j + 1) * 128, :], ("wd", f, j))

    def wunit(ap2d, keys):
        def mk(slot):
            if len(ap2d.shape) == 3:
                o = slot[:].rearrange("p (a b) -> p a b", a=ap2d.shape[1])
            else:
                o = slot[:]
            return [(o, ap2d, {})]
        return WS.get(mk, [sres(k) for k in keys])

    def mm(ps, M, N0, N1, lhsT, rhs, start, stop, reads, writes_res, kpart=None):
        t = ps[0]
        return P.pe(lambda e: e.matmul(t[0:M, N0:N1], lhsT, rhs, start=start, stop=stop),
                    reads=reads, writes=[writes_res])

    class TileInfo:
        pass

    def make_tile(kind, b=0, i=0):
        ti = TileInfo()
        ti.kind = kind
        ti.b = b
        ti.i = i
        if kind == "spec":
            ti.n = NSPEC
            ti.subs = [NSPEC]
            ti.segs = [(0, NMETA, "zero"), (NMETA, DEC, "c0"), (NMETA + DEC, DEC, "c1")]
            ti.col0 = 0
            ti.vblk = [(0, NMETA), (NMETA, DEC), (NMETA + DEC, DEC)]
        else:
            ti.n = T
            ti.subs = [128] * 4
            ti.segs = [(0, T, "meta" if i == 0 else "prev")]
            ti.col0 = NSPEC + i * T
            ti.vblk = [(s * 128, 128) for s in range(4)]
        return ti

    def rms_to_hT(ti, gcol):
        xt, R_xt = X["t"], X["R"]
        n = ti.n
        ns = len(ti.subs)
        nr0 = ti.subs[0]
        P.pool(lambda e: e.memset(ss4[:, 0:4], 0.0), writes=[R_ss4])
        for s, nr in enumerate(ti.subs):
            P.act(lambda e, s=s, nr=nr: e.activation(out=junk[0:nr, 0:D], in_=xt[0:nr, s, :], func=AF.Square,
                                                     accum_out=ss4[0:nr, s:s + 1]),
                  reads=[R_xt[s][0], R_xt[s][1]], writes=[R_junk, R_ss4])
        P.pool(lambda e: e.tensor_scalar(out=ss4[0:nr0, 4:4 + ns], in0=ss4[0:nr0, 0:ns], scalar1=1.0 / D, scalar2=EPS,
                                         op0=ALU.mult, op1=ALU.add), reads=[R_ss4], writes=[R_ss4])
        P.pool(lambda e: e.tensor_tensor(out=ss4[0:nr0, 0:ns], in0=ss4[0:nr0, 4:4 + ns], in1=mhalf[0:nr0, 0:ns],
                                         op=ALU.pow), reads=[R_ss4, R_const], writes=[R_ss4])
        banks = [PA.get() for _ in range(6)] + [PB.get() for _ in range(2)]
        for s, nr in enumerate(ti.subs):
            xb, xr = xnR.get()
            P.dve(lambda e, s=s, nr=nr, xb=xb: e.tensor_scalar(out=xb[0:nr, :], in0=xt[0:nr, s, :],
                                                               scalar1=ss4[0:nr, s:s + 1], scalar2=None,
                                                               op0=ALU.mult),
                  reads=[R_xt[s][0], R_xt[s][1], R_ss4], writes=[xr])
            for kc in range(8):
                bk = banks[kc]
                mm(bk, 128, s * 128, s * 128 + nr, xb[0:nr, kc * 128:(kc + 1) * 128], ident[0:nr, 0:nr],
                   True, True, [xr, R_const], bk[1])
        for kc in range(8):
            bk = banks[kc]
            if kc % 2 == 0:
                P.dve(lambda e, bk=bk, kc=kc: e.tensor_scalar(
                    out=hT[:, kc, 0:n], in0=bk[0][:, 0:n], scalar1=gT[:, gcol + kc:gcol + kc + 1],
                    scalar2=None, op0=ALU.mult), reads=[bk[1], R_const], writes=[R_hT[kc]])
            else:
                P.act(lambda e, bk=bk, kc=kc: e.activation(
                    out=hT[:, kc, 0:n], in_=bk[0][:, 0:n], func=AF.Copy,
                    scale=gT[:, gcol + kc:gcol + kc + 1]), reads=[bk[1], R_const], writes=[R_hT[kc]])
        for kc in range(6):
            PA.free(banks[kc])
        PB.free(banks[6])
        PB.free(banks[7])

    def ffn(ti, f, pre=False, mid_hook=None):
        xt, R_xt = X["t"], X["R"]
        n = ti.n
        if not pre:
            rms_to_hT(ti, 0 if f == 0 else 16)
        for j in range(NJ):
            ug, rg, ng = wunit(wg_s[f][j], [("wg", f, j)])
            uu, ru, nu = wunit(wu_s[f][j], [("wu", f, j)])
            pg = PA.get()
            pu = PA.get()
            ugv = ug[:].rearrange("p (k m) -> p k m", k=8)
            uuv = uu[:].rearrange("p (k m) -> p k m", k=8)
            for kc in range(8):
                mm(pg, 128, 0, n, ugv[:, kc, :], hT[:, kc, 0:n], kc == 0, kc == 7, [rg, R_hT[kc]], pg[1])
            WS.done(ng)
            for kc in range(8):
                mm(pu, 128, 0, n, uuv[:, kc, :], hT[:, kc, 0:n], kc == 0, kc == 7, [ru, R_hT[kc]], pu[1])
            WS.done(nu)
            tb, tr = tmpR.get()
            P.act(lambda e, tb=tb, pg=pg: e.activation(out=tb[:, 0:n], in_=pg[0][:, 0:n], func=AF.Silu),
                  reads=[pg[1]], writes=[tr])
            P.dve(lambda e, tb=tb, pu=pu, j=j: e.tensor_tensor(out=aT[:, j, 0:n], in0=pu[0][:, 0:n],
                                                               in1=tb[:, 0:n], op=ALU.mult),
                  reads=[pu[1], tr], writes=[R_aT[j]] + ((R_kT + R_qT) if j == 0 else []))
            PA.free(pg)
            PA.free(pu)
        if mid_hook is not None:
            mid_hook()
        for half in range(2):
            pd = [PA.get() for _ in ti.subs]
            for jj in range(NJ // 2):
                src = wd_s[f][jj * 256:(jj + 1) * 256, half * T:(half + 1) * T].rearrange("(t p) n -> p t n", p=128)
                u, ru, nu = wunit(src, [("wd", f, 2 * jj), ("wd", f, 2 * jj + 1)])
                uv = u[:].rearrange("p (t n) -> p t n", t=2)
                for t2 in range(2):
                    j = 2 * jj + t2
                    for s, nr in enumerate(ti.subs):
                        mm(pd[s], nr, 0, T, aT[:, j, s * 128:s * 128 + nr], uv[:, t2, :], j == 0, j == NJ - 1,
                           [ru, R_aT[j]], pd[s][1])
                WS.done(nu)
            for s, nr in enumerate(ti.subs):
                P.dve(lambda e, s=s, nr=nr, pd=pd, half=half: e.scalar_tensor_tensor(
                    out=xt[0:nr, s, half * T:(half + 1) * T], in0=pd[s][0][0:nr, :], scalar=0.5,
                    in1=xt[0:nr, s, half * T:(half + 1) * T], op0=ALU.mult, op1=ALU.add),
                    reads=[pd[s][1], R_xt[s][half]], writes=[R_xt[s][half]])
                PA.free(pd[s])

    def rstd_from(ps, M, n, dim):
        lb, lr = tmpR.get()
        P.act(lambda e: e.activation(out=lb[0:M, 0:n], in_=ps[0][0:M, 0:n], func=AF.Ln, scale=1.0 / dim, bias=EPS),
              reads=[ps[1]], writes=[lr])
        rb, rr = tmpR.get()
        P.act(lambda e: e.activation(out=rb[0:M, 0:n], in_=lb[0:M, 0:n], func=AF.Exp, scale=-0.5),
              reads=[lr], writes=[rr])
        return rb, rr

    def make_k_stages(n, pssP):
        stash = {}

        def k_stage1(h):
            pk = PA.get()
            mm(pk, 128, 0, n, wka[:, h, :], ckvT[:, 0:n], True, False, [R_ckvT, R_const], pk[1])
            mm(pk, 128, 0, n, bsel[:, :], kpeT[:, 0:n], False, True, [R_kpeT, R_const], pk[1])
            sq, sr = sqhR.get()
            P.act(lambda e, sq=sq, pk=pk: e.activation(out=sq[:, 0:n], in_=pk[0][0:96, 0:n], func=AF.Square),
                  reads=[pk[1]], writes=[sr])
            stash[h] = (pk, sq, sr)

        def k_stage2(h):
            pk, sq, sr = stash.pop(h)
            pss = pssP.get()
            mm(pss, 128, 0, n, ones[0:96, :], sq[:, 0:n], True, True, [sr, R_const], pss[1])
            rb, rr = rstd_from(pss, 96, n, 96)
            pssP.free(pss)
            P.dve(lambda e, pk=pk, rb=rb, h=h: e.scalar_tensor_tensor(
                out=kTn[0:96, h, 0:n], in0=pk[0][0:96, 0:n], scalar=gk[:, 0:1], in1=rb[0:96, 0:n],
                op0=ALU.mult, op1=ALU.mult), reads=[pk[1], rr, R_const], writes=[R_kT[h]])
            PA.free(pk)

        return k_stage1, k_stage2

    def kv_prep(n, vblk, pssP=None):
        k_stage1, k_stage2 = make_k_stages(n, pssP or PB)
        LK = 2
        for h in range(NH + LK):
            if h < NH:
                k_stage1(h)
            if h - LK >= 0:
                k_stage2(h - LK)
        v_part(n, vblk)

    def v_part(n, vblk):
        for bi, (c0, ln) in enumerate(vblk):
            for half in range(2):
                pv = PA.get()
                mm(pv, ln, 0, T, ckvT[:, c0:c0 + ln], wuv[:, half * T:(half + 1) * T], True, True,
                   [R_ckvT, R_const], pv[1])
                dst = vn[0:ln, half * 8:(half + 1) * 8, bi, :].rearrange("p (a b) c -> p a b c", b=2)
                srcv = pv[0][0:ln, :].rearrange("p (a b d) -> p a b d", a=4, b=2)
                for par in range(2):
                    if (half + par) % 2 == 0:
                        P.act(lambda e, dst=dst, srcv=srcv, par=par: e.activation(
                            out=dst[:, :, par, par * 64:par * 64 + 64], in_=srcv[:, :, par, :], func=AF.Copy),
                            reads=[pv[1]], writes=[R_vn])
                    else:
                        P.dve(lambda e, dst=dst, srcv=srcv, par=par: e.tensor_copy(
                            out=dst[:, :, par, par * 64:par * 64 + 64], in_=srcv[:, :, par, :]),
                            reads=[pv[1]], writes=[R_vn])
                PA.free(pv)

    def kv_transpose(subs):
        b1 = PA.get()
        b2 = PA.get()
        t1 = b1[0][:].bitcast(BF16)
        t2 = b2[0][:].bitcast(BF16)
        n = 0
        for s, nr in enumerate(subs):
            P.pe(lambda e, s=s, nr=nr: e.transpose(t1[:, s * 128:s * 128 + nr], kvb[0:nr, s, 0:128], ident[0:nr, 0:nr]),
                 reads=[R_kvb, R_const], writes=[b1[1]])
            P.pe(lambda e, s=s, nr=nr: e.transpose(t2[0:32, s * 128:s * 128 + nr], kvb[0:nr, s, 128:160],
                                                   ident[0:nr, 0:nr]),
                 reads=[R_kvb, R_const], writes=[b2[1]])
            n = s * 128 + nr
        P.act(lambda e: e.activation(out=ckvT[:, 0:n], in_=t1[:, 0:n], func=AF.Copy), reads=[b1[1]], writes=[R_ckvT])
        P.dve(lambda e: e.tensor_copy(out=kpeT[:, 0:n], in_=t2[0:32, 0:n]), reads=[b2[1]], writes=[R_kpeT])
        PA.free(b1)
        PA.free(b2)

    def in_all(ti):
        n = ti.n
        rms_to_hT(ti, 8)
        P.dma("pool", ch_io, cs_f[64:128, 0:n], csfm_in[64:128, ti.col0:ti.col0 + n], writes=[R_csf])
        if ti.kind == "spec":
            P.dma("pool", ch_io, cs_t[0:n, 0, :], cstm_in[0:n, :], writes=[R_cst])
        else:
            P.dma("pool", ch_io, cs_t[:], cstm_in[ti.col0:ti.col0 + n, :].rearrange("(s p) c -> p s c", p=128),
                  writes=[R_cst])
        def conv_chunk(jc):
            uc, rc, n1 = wunit(win_s[U_C + jc], [("win", U_C + jc)])
            uv_, rv, n2 = wunit(win_s[U_V + jc], [("win", U_V + jc)])
            ub, rb_, n3 = wunit(win_s[U_B + jc], [("win", U_B + jc)])
            pc = PA.get()
            pv = PA.get()
            pb = PA.get()
            for (u, ru, pp, nn) in ((uc, rc, pc, n1), (uv_, rv, pv, n2), (ub, rb_, pb, n3)):
                uvw = u[:].rearrange("p (k m) -> p k m", k=8)
                for kc in range(8):
                    mm(pp, 128, 0, n, uvw[:, kc, :], hT[:, kc, 0:n], kc == 0, kc == 7, [ru, R_hT[kc]], pp[1])
                WS.done(nn)
            tb, tr = tmpR.get()
            P.act(lambda e, tb=tb, pc=pc: e.activation(out=tb[:, 0:n], in_=pc[0][:, 0:n], func=AF.Copy),
                  reads=[pc[1]], writes=[tr])
            PA.free(pc)
            cb, cr = cinR.get()
            yb, yr = tmpR.get()
            for si, (c0, ln, hk) in enumerate(ti.segs):
                o0 = c0 + 2 * si
                if hk == "zero":
                    P.pool(lambda e, cb=cb, o0=o0: e.memset(cb[:, o0:o0 + 2], 0.0), writes=[cr])
                elif hk in ("c0", "c1"):
                    sbi = 0 if hk == "c0" else 1
                    P.pool(lambda e, cb=cb, o0=o0, sbi=sbi, jc=jc: e.tensor_copy(out=cb[:, o0:o0 + 2], in_=cch[:, jc, sbi, :]),
                           reads=[R_const], writes=[cr])
                elif hk == "meta":
                    P.pool(lambda e, cb=cb, o0=o0, jc=jc: e.tensor_copy(out=cb[:, o0:o0 + 2], in_=halo_meta[:, jc, :]),
                           reads=[R_halo_meta], writes=[cr])
                else:
                    P.pool(lambda e, cb=cb, o0=o0, jc=jc: e.tensor_copy(out=cb[:, o0:o0 + 2], in_=halo[:, jc, :]),
                           reads=[R_halo], writes=[cr])
                P.dve(lambda e, cb=cb, o0=o0, c0=c0, ln=ln, tb=tb, pv=pv: e.tensor_tensor(
                    out=cb[:, o0 + 2:o0 + 2 + ln], in0=pv[0][:, c0:c0 + ln], in1=tb[:, c0:c0 + ln], op=ALU.mult),
                    reads=[pv[1], tr], writes=[cr])
                P.dve(lambda e, cb=cb, o0=o0, c0=c0, ln=ln, yb=yb, jc=jc: e.tensor_scalar(
                    out=yb[:, c0:c0 + ln], in0=cb[:, o0:o0 + ln], scalar1=cw[:, jc, 0:1], scalar2=None, op0=ALU.mult),
                    reads=[cr, R_const], writes=[yr])
                for k in (1, 2):
                    P.dve(lambda e, cb=cb, o0=o0, c0=c0, ln=ln, yb=yb, jc=jc, k=k: e.scalar_tensor_tensor(
                        out=yb[:, c0:c0 + ln], in0=cb[:, o0 + k:o0 + k + ln], scalar=cw[:, jc, k:k + 1],
                        in1=yb[:, c0:c0 + ln], op0=ALU.mult, op1=ALU.add), reads=[cr, yr, R_const], writes=[yr])
                if ti.kind == "spec":
                    if hk == "zero":
                        P.pool(lambda e, cb=cb, o0=o0, ln=ln, jc=jc: e.tensor_copy(
                            out=halo_meta[:, jc, :], in_=cb[:, o0 + ln:o0 + ln + 2]), reads=[cr], writes=[R_halo_meta])
                    else:
                        sbi = 0 if hk == "c0" else 1
                        P.pool(lambda e, cb=cb, o0=o0, ln=ln, jc=jc, sbi=sbi: e.tensor_copy(
                            out=ncs[:, jc, sbi, :], in_=cb[:, o0 + ln:o0 + ln + 2]), reads=[cr], writes=[R_ncs])
                else:
                    P.pool(lambda e, cb=cb, o0=o0, ln=ln, jc=jc: e.tensor_copy(
                        out=halo[:, jc, :], in_=cb[:, o0 + ln:o0 + ln + 2]), reads=[cr], writes=[R_halo])
            PA.free(pv)
            P.dve(lambda e, yb=yb, pb=pb, jc=jc: e.tensor_tensor(out=gated[:, jc, 0:n], in0=pb[0][:, 0:n],
                                                                 in1=yb[:, 0:n], op=ALU.mult),
                  reads=[pb[1], yr], writes=[R_gated[jc]])
            PA.free(pb)
        qlfb = [tmpR.get() for _ in range(3)]
        for c in range(3):
            u, ru, nn = wunit(win_s[U_Q + c], [("win", U_Q + c)])
            pq = PA.get()
            uvw = u[:].rearrange("p (k m) -> p k m", k=8)
            for kc in range(8):
                mm(pq, 128, 0, n, uvw[:, kc, :], hT[:, kc, 0:n], kc == 0, kc == 7, [ru, R_hT[kc]], pq[1])
            WS.done(nn)
            P.act(lambda e, pq=pq, c=c: e.activation(out=qlfb[c][0][:, 0:n], in_=pq[0][:, 0:n], func=AF.Copy),
                  reads=[pq[1]], writes=[qlfb[c][1]])
            P.act(lambda e, pq=pq, c=c: e.activation(out=sqb[:, c, 0:n], in_=pq[0][:, 0:n], func=AF.Square),
                  reads=[pq[1]], writes=[R_sqb])
            PA.free(pq)
        pss = PA.get()
        for c in range(3):
            mm(pss, 128, 0, n, ones[:, :], sqb[:, c, 0:n], c == 0, c == 2, [R_sqb, R_const], pss[1])
        rb, rr = rstd_from(pss, 128, n, 384)
        PA.free(pss)
        for c in range(3):
            P.dve(lambda e, c=c, rb=rb: e.scalar_tensor_tensor(
                out=qlT[:, c, 0:n], in0=qlfb[c][0][:, 0:n], scalar=gqa[:, c:c + 1], in1=rb[:, 0:n],
                op0=ALU.mult, op1=ALU.mult), reads=[qlfb[c][1], rr, R_const], writes=[R_qlT])
        ns = len(ti.subs)
        nr0 = ti.subs[0]
        P.pool(lambda e: e.memset(ss4[:, 0:4], 0.0), writes=[R_ss4])
        for s, nr in enumerate(ti.subs):
            pkv = PA.get()
            for kc in range(8):
                mm(pkv, nr, 0, 160, hT[:, kc, s * 128:s * 128 + nr], wkv[:, kc, :], kc == 0, kc == 7,
                   [R_hT[kc], R_const], pkv[1])
            P.act(lambda e, s=s, nr=nr, pkv=pkv: e.activation(out=kvt[0:nr, s, :], in_=pkv[0][0:nr, 0:160], func=AF.Copy),
                  reads=[pkv[1]], writes=[R_kvt])
            P.act(lambda e, s=s, nr=nr, pkv=pkv: e.activation(out=junk[0:nr, 0:128], in_=pkv[0][0:nr, 0:128],
                                                              func=AF.Square, accum_out=ss4[0:nr, s:s + 1]),
                  reads=[pkv[1]], writes=[R_junk, R_ss4])
            PA.free(pkv)
        P.act(lambda e: e.activation(out=ss4[0:nr0, 4:4 + ns], in_=ss4[0:nr0, 0:ns], func=AF.Ln,
                                     scale=1.0 / 128, bias=EPS), reads=[R_ss4], writes=[R_ss4])
        P.act(lambda e: e.activation(out=ss4[0:nr0, 0:ns], in_=ss4[0:nr0, 4:4 + ns], func=AF.Exp, scale=-0.5),
              reads=[R_ss4], writes=[R_ss4])
        for s, nr in enumerate(ti.subs):
            P.dve(lambda e, s=s, nr=nr: e.scalar_tensor_tensor(
                out=ckv_tm[0:nr, s, :], in0=kvt[0:nr, s, 0:128], scalar=ss4[0:nr, s:s + 1], in1=gkv[0:nr, :],
                op0=ALU.mult, op1=ALU.mult), reads=[R_kvt, R_ss4, R_const], writes=[R_ckvtm])
        xv = kvt[0:nr0, 0:ns, 128:160]
        P.dve(lambda e: e.tensor_tensor(out=rt1[0:nr0, 0:ns, :], in0=xv, in1=cs_t[0:nr0, 0:ns, 0:32], op=ALU.mult),
              reads=[R_kvt, R_cst], writes=[R_rt])
        P.dve(lambda e: e.tensor_tensor(out=rt2[0:nr0, 0:ns, :], in0=xv, in1=cs_t[0:nr0, 0:ns, 32:64], op=ALU.mult),
              reads=[R_kvt, R_cst], writes=[R_rt])
        P.dve(lambda e: e.tensor_tensor(out=kpe_tm[0:nr0, 0:ns, 0:16], in0=rt1[0:nr0, 0:ns, 0:16],
                                        in1=rt2[0:nr0, 0:ns, 16:32], op=ALU.subtract),
              reads=[R_rt], writes=[R_kpetm])
        P.dve(lambda e: e.tensor_tensor(out=kpe_tm[0:nr0, 0:ns, 16:32], in0=rt1[0:nr0, 0:ns, 16:32],
                                        in1=rt2[0:nr0, 0:ns, 0:16], op=ALU.add),
              reads=[R_rt], writes=[R_kpetm])
        P.pool(lambda e: e.tensor_copy(out=kvb[0:nr0, 0:ns, 0:128], in_=ckv_tm[0:nr0, 0:ns, :]),
               reads=[R_ckvtm], writes=[R_kvb])
        P.pool(lambda e: e.tensor_copy(out=kvb[0:nr0, 0:ns, 128:160], in_=kpe_tm[0:nr0, 0:ns, :]),
               reads=[R_kpetm], writes=[R_kvb])
        if ti.kind == "spec":
            for b in range(2):
                P.dma("pool", ch_io, nkv_p[b, 0:NMETA, :], ckv_tm[0:NMETA, 0, :], reads=[R_ckvtm])
                P.dma("pool", ch_io, nkr_p[b, 0:NMETA, :], kpe_tm[0:NMETA, 0, :], reads=[R_kpetm])
                r0 = NMETA + DEC * b
                P.dma("pool", ch_io, nkv_s[b], ckv_tm[r0:r0 + DEC, 0, :], reads=[R_ckvtm])
                P.dma("pool", ch_io, nkr_s[b], kpe_tm[r0:r0 + DEC, 0, :], reads=[R_kpetm])
        else:
            r0 = NMETA + ti.i * T
            P.dma("pool", ch_io, nkv_p[ti.b, r0:r0 + T, :].rearrange("(s p) c -> p s c", p=128), ckv_tm[:],
                  reads=[R_ckvtm])
            P.dma("pool", ch_io, nkr_p[ti.b, r0:r0 + T, :].rearrange("(s p) c -> p s c", p=128), kpe_tm[:],
                  reads=[R_kpetm])
        kv_transpose(ti.subs)
        LQ = 2
        qst = {}

        def q_stage1(h):
            pq = PA.get()
            for c in range(3):
                mm(pq, 128, 0, n, wq[:, c, h, :], qlT[:, c, 0:n], c == 0, c == 2, [R_qlT, R_const], pq[1])
            qb, qr = qbR.get()
            t2, t2r = tmpR.get()
            P.act(lambda e, qb=qb, pq=pq: e.activation(out=qb[0:96, 0:n], in_=pq[0][0:96, 0:n], func=AF.Copy),
                  reads=[pq[1]], writes=[qr])
            P.dve(lambda e, qb=qb, pq=pq: e.tensor_tensor(out=qb[64:96, 0:n], in0=pq[0][64:96, 0:n],
                                                          in1=cs_f[64:96, 0:n], op=ALU.mult),
                  reads=[pq[1], R_csf], writes=[qr])
            P.dve(lambda e, t2=t2, pq=pq: e.tensor_tensor(out=t2[64:96, 0:n], in0=pq[0][96:128, 0:n],
                                                          in1=cs_f[96:128, 0:n], op=ALU.mult),
                  reads=[pq[1], R_csf], writes=[t2r])
            PA.free(pq)
            P.dve(lambda e, qb=qb, t2=t2: e.tensor_tensor(out=qb[64:96, 0:n], in0=qb[64:96, 0:n], in1=t2[64:96, 0:n],
                                                          op=ALU.add), reads=[qr, t2r], writes=[qr])
            sq, sr = sqhR.get()
            P.pool(lambda e, qb=qb, sq=sq: e.tensor_tensor(out=sq[:, 0:n], in0=qb[0:96, 0:n], in1=qb[0:96, 0:n],
                                                           op=ALU.mult), reads=[qr], writes=[sr])
            qst[h] = (qb, qr, sq, sr)

        def q_stage2(h):
            qb, qr, sq, sr = qst.pop(h)
            pss = PB.get()
            mm(pss, 128, 0, n, ones[0:96, :], sq[:, 0:n], True, True, [sr, R_const], pss[1])
            rb, rr = rstd_from(pss, 96, n, 96)
            PB.free(pss)
            P.dve(lambda e, qb=qb, rb=rb, h=h: e.scalar_tensor_tensor(
                out=qT[0:96, h, 0:n], in0=qb[0:96, 0:n], scalar=gq[:, 0:1], in1=rb[0:96, 0:n],
                op0=ALU.mult, op1=ALU.mult), reads=[qr, rr, R_const], writes=[R_qT[h]])

        k_stage1, k_stage2 = make_k_stages(n, PB)
        for g in range(8):
            for h in (2 * g, 2 * g + 1):
                q_stage1(h)
                k_stage1(h)
                if h >= 2:
                    q_stage2(h - 2)
                    k_stage2(h - 2)
            conv_chunk(g)
        for h in (NH - 2, NH - 1):
            q_stage2(h)
            k_stage2(h)
        v_part(n, ti.vblk)

    LAG = 3

    def attention(ti):
        b, i = ti.b, ti.i
        KVS.limit = KVS.n + 2 * i * NH
        for h in range(NH):
            po = PB.get()
            blocks = [("meta", kT_meta[:, h, :], v_meta[:, h, :], NMETA, 0, None, [R_kTm], [R_vm], None)]
            for j in range(i):
                for t4 in range(4):
                    blocks.append(("prev", j, t4, 128, 0, None, None, None, None))
            for r in range(4):
                blocks.append(("diag", kTn[0:96, h, r * 128:(r + 1) * 128], vn[:, h, r, :], 128, r * 128, r,
                               [R_kT[h]], [R_vn], None))
            cur = {}

            def resolve(blk, h=h):
                if blk[0] != "prev":
                    return blk
                j, t4 = blk[1], blk[2]
                if t4 == 0:
                    kslot, kres, kn = KVS.get(lambda slot, j=j: [(slot[0:96, 0:T], kS[b, j, h], {})],
                                              [sres(("kS", b, j))])
                    vslot, vres, vn_ = KVS.get(lambda slot, j=j: [(slot[:, 0:T], vS[b, j, h], {})],
                                               [sres(("vS", b, j))])
                    cur["u"] = (kslot, kres, kn, vslot, vres, vn_)
                kslot, kres, kn, vslot, vres, vn_ = cur["u"]
                vv = vslot[:, 0:T].rearrange("p (t c) -> p t c", t=4)
                return ("prev", kslot[0:96, t4 * 128:(t4 + 1) * 128], vv[:, t4, :], 128, 0, None,
                        [kres], [vres], (kn, vn_) if t4 == 3 else None)
            nb = len(blocks)
            pend = []

            def do_pv(idx, blk, pt, ptr):
                kind, kap, vap, K, c0, r, kr, vr, rel = blk
                first = idx == 0
                last = idx == nb - 1
                if kind != "diag":
                    mm(po, 128, c0, T, vap[0:K, :], pt[0:K, c0:T], first, last, vr + [ptr], po[1])
                else:
                    mm(po, 128, c0, c0 + 64, vap[0:64, :], pt[0:64, c0:c0 + 64], False, False, vr + [ptr], po[1])
                    mm(po, 128, c0 + 64, T, vap[0:128, :], pt[0:128, c0 + 64:T], False, last, vr + [ptr], po[1])
                if rel is not None:
                    KVS.done(rel[0])
                    KVS.done(rel[1])

            for idx, blk in enumerate(blocks):
                blk = resolve(blk)
                kind, kap, vap, K, c0, r, kr, vr, rel = blk
                ps = PA.get()
                mm(ps, K, c0, T, kap, qT[0:96, h, c0:T], True, True, kr + [R_qT[h]], ps[1])
                pt, ptr = pTR.get()
                P.act(lambda e, ps=ps, pt=pt, K=K, c0=c0: e.activation(out=pt[0:K, c0:T], in_=ps[0][0:K, c0:T], func=AF.Exp),
                      reads=[ps[1]], writes=[ptr])
                PA.free(ps)
                pend.append((idx, blk, pt, ptr))
                if len(pend) > LAG:
                    do_pv(*pend.pop(0))
            while pend:
                do_pv(*pend.pop(0))
            rb, rr = tmpR.get()
            c = h // 2
            if h % 2 == 0:
                P.dve(lambda e, rb=rb, po=po: e.reciprocal(out=rb[0:64, :], in_=po[0][64:128, :]), reads=[po[1]], writes=[rr])
                P.dve(lambda e, rb=rb, po=po, c=c: e.tensor_tensor(out=attnT[0:64, c, :], in0=po[0][0:64, :],
                                                                   in1=rb[0:64, :], op=ALU.mult),
                      reads=[po[1], rr], writes=[R_attn[c]])
            else:
                P.dve(lambda e, rb=rb, po=po: e.reciprocal(out=rb[64:128, :], in_=po[0][0:64, :]), reads=[po[1]], writes=[rr])
                P.dve(lambda e, rb=rb, po=po, c=c: e.tensor_tensor(out=attnT[64:128, c, :], in0=po[0][64:128, :],
                                                                   in1=rb[64:128, :], op=ALU.mult),
                      reads=[po[1], rr], writes=[R_attn[c]])
            PB.free(po)

    def merge(ti):
        xt, R_xt = X["t"], X["R"]
        n = ti.n
        for m in range(8):
            u1, r1, n1 = wunit(wco_s[m], [("wco", m)])
            u2, r2, n2 = wunit(win_s[U_GC + m], [("win", U_GC + m)])
            u3, r3, n3 = wunit(wmo_s[m], [("wmo", m)])
            u4, r4, n4 = wunit(win_s[U_GM + m], [("win", U_GM + m)])
            pcb = PA.get()
            pgc = PA.get()
            pmb = PA.get()
            pgm = PA.get()
            for (u, ru, pp, nn, rhs, rres) in ((u1, r1, pcb, n1, gated, R_gated), (u2, r2, pgc, n2, hT, R_hT),
                                                (u3, r3, pmb, n3, attnT, R_attn), (u4, r4, pgm, n4, hT, R_hT)):
                uvw = u[:].rearrange("p (k m) -> p k m", k=8)
                for kc in range(8):
                    mm(pp, 128, 0, n, uvw[:, kc, :], rhs[:, kc, 0:n], kc == 0, kc == 7, [ru, rres[kc]], pp[1])
                WS.done(nn)
            s1, s1r = tmpR.get()
            s2, s2r = tmpR.get()
            P.act(lambda e, s1=s1, pgc=pgc: e.activation(out=s1[:, 0:n], in_=pgc[0][:, 0:n], func=AF.Sigmoid),
                  reads=[pgc[1]], writes=[s1r])
            P.act(lambda e, s2=s2, pgm=pgm: e.activation(out=s2[:, 0:n], in_=pgm[0][:, 0:n], func=AF.Sigmoid),
                  reads=[pgm[1]], writes=[s2r])
            PA.free(pgc)
            PA.free(pgm)
            P.dve(lambda e, s1=s1, pcb=pcb: e.tensor_tensor(out=s1[:, 0:n], in0=pcb[0][:, 0:n], in1=s1[:, 0:n], op=ALU.mult),
                  reads=[pcb[1], s1r], writes=[s1r])
            P.dve(lambda e, s2=s2, pmb=pmb: e.tensor_tensor(out=s2[:, 0:n], in0=pmb[0][:, 0:n], in1=s2[:, 0:n], op=ALU.mult),
                  reads=[pmb[1], s2r], writes=[s2r])
            PA.free(pcb)
            PA.free(pmb)
            P.pool(lambda e, s1=s1, s2=s2, m=m: e.tensor_tensor(out=mergedT[:, m, 0:n], in0=s1[:, 0:n], in1=s2[:, 0:n],
                                                                op=ALU.add), reads=[s1r, s2r], writes=[R_merged[m]])
        for half in range(2):
            pd = [PA.get() for _ in ti.subs]
            for kk in range(4):
                src = woa_s[kk * 256:(kk + 1) * 256, half * T:(half + 1) * T].rearrange("(t p) n -> p t n", p=128)
                u, ru, nu = wunit(src, [("woa", 2 * kk), ("woa", 2 * kk + 1)])
                uv = u[:].rearrange("p (t n) -> p t n", t=2)
                for t2 in range(2):
                    kc = 2 * kk + t2
                    for s, nr in enumerate(ti.subs):
                        mm(pd[s], nr, 0, T, mergedT[:, kc, s * 128:s * 128 + nr], uv[:, t2, :], kc == 0, kc == 7,
                           [ru, R_merged[kc]], pd[s][1])
                WS.done(nu)
            for s, nr in enumerate(ti.subs):
                P.dve(lambda e, s=s, nr=nr, pd=pd, half=half: e.tensor_tensor(
                    out=xt[0:nr, s, half * T:(half + 1) * T], in0=pd[s][0][0:nr, :],
                    in1=xt[0:nr, s, half * T:(half + 1) * T], op=ALU.add),
                    reads=[pd[s][1], R_xt[s][half]], writes=[R_xt[s][half]])
                PA.free(pd[s])

    def sample_attention():
        pos = [PB.get(), PB.get()]

        def block_pass(sbi, ktile_aps, first, last_block):
            q0 = NMETA + DEC * sbi
            nbk = len(ktile_aps)
            for bi, (kf, vf, K, krf, vres) in enumerate(ktile_aps):
                ps = PA.get()
                for h in range(NH):
                    mm(ps, K, h * DEC, (h + 1) * DEC, kf(h), qT[0:96, h, q0:q0 + DEC], True, True,
                       [krf(h), R_qT[h]], ps[1])
                pt, ptr = pTR.get()
                P.act(lambda e, ps=ps, pt=pt, K=K: e.activation(out=pt[0:K, :], in_=ps[0][0:K, :], func=AF.Exp),
                      reads=[ps[1]], writes=[ptr])
                PA.free(ps)
                for h in range(NH):
                    mm(pos[sbi], 128, h * DEC, (h + 1) * DEC, vf(h), pt[0:K, h * DEC:(h + 1) * DEC],
                       first and bi == 0 and h == 0, last_block and bi == nbk - 1 and h == NH - 1,
                       [vres, ptr], pos[sbi][1])

        for sbi in range(2):
            q0 = NMETA + DEC * sbi
            block_pass(sbi, [(lambda h, q0=q0: kTn[0:96, h, q0:q0 + DEC],
                              lambda h, sbi=sbi: vn[0:DEC, h, 1 + sbi, :], DEC, lambda h: R_kT[h], R_vn)], True, False)
        for sbi in range(2):
            for cb in range(PAST // T):
                r0 = cb * T
                P.dma("pool", ch_io, stage[:, :, 0:128], c_kv[sbi, r0:r0 + T, :].rearrange("(s p) c -> p s c", p=128),
                      writes=[R_stage])
                P.dma("pool", ch_io, stage[:, :, 128:160], c_kr[sbi, r0:r0 + T, :].rearrange("(s p) c -> p s c", p=128),
                      writes=[R_stage])
                P.pool(lambda e: e.tensor_copy(out=kvb[:], in_=stage[:]), reads=[R_stage], writes=[R_kvb])
                kv_transpose([128] * 4)
                kv_prep(T, [(s * 128, 128) for s in range(4)], PA)
                block_pass(sbi, [(lambda h, t4=t4: kTn[0:96, h, t4 * 128:(t4 + 1) * 128],
                                  lambda h, t4=t4: vn[:, h, t4, :], 128, lambda h: R_kT[h], R_vn) for t4 in range(4)],
                           False, cb == PAST // T - 1)
        for sbi in range(2):
            q0 = NMETA + DEC * sbi
            po = pos[sbi]
            pv4 = po[0][:].rearrange("p (a b q) -> p a b q", a=8, b=2)
            rb, rr = tmpR.get()
            rb4 = rb[:, 0:256].rearrange("p (a q) -> p a q", a=8)
            P.dve(lambda e, rb4=rb4, pv4=pv4: e.reciprocal(out=rb4[0:64], in_=pv4[64:128, :, 0, :]), reads=[po[1]], writes=[rr])
            P.dve(lambda e, rb4=rb4, pv4=pv4: e.reciprocal(out=rb4[64:128], in_=pv4[0:64, :, 1, :]), reads=[po[1]], writes=[rr])
            P.dve(lambda e, rb4=rb4, pv4=pv4, q0=q0: e.tensor_tensor(out=attnT[0:64, :, q0:q0 + DEC], in0=pv4[0:64, :, 0, :],
                                                                     in1=rb4[0:64], op=ALU.mult),
                  reads=[po[1], rr], writes=R_attn)
            P.dve(lambda e, rb4=rb4, pv4=pv4, q0=q0: e.tensor_tensor(out=attnT[64:128, :, q0:q0 + DEC], in0=pv4[64:128, :, 1, :],
                                                                     in1=rb4[64:128], op=ALU.mult),
                  reads=[po[1], rr], writes=R_attn)
            PB.free(po)

    def axt(k):
        return [xbufs[k % 2][1][s_][h_] for s_ in range(4) for h_ in range(2)]

    tiles = [make_tile("spec")] + [make_tile("reg", b, i) for b in range(2) for i in range(NT)]
    if stg < 99:
        tiles = tiles[:max(1, stg)]

    def load_x(k):
        ti = tiles[k]
        xt_k = xbufs[k % 2][0]
        if ti.kind == "spec":
            P.dma("pool", ch_io, xt_k[0:NMETA, 0, :], meta, writes=axt(k))
            for sbi in range(2):
                r0 = NMETA + DEC * sbi
                P.dma("pool", ch_io, xt_k[r0:r0 + DEC, 0, :], x_s[sbi], writes=axt(k))
        else:
            P.dma("pool", ch_io, xt_k[:], x_p[ti.b, ti.i * T:(ti.i + 1) * T, :].rearrange("(s p) d -> p s d", p=128),
                  writes=axt(k))

    def use(k):
        X["t"], X["R"] = xbufs[k % 2]

    load_x(0)
    cast_ffn(0)
    for u in range(NWIN):
        cast(win_s[u], win_in[u], ("win", u))
    for m in range(8):
        cast(wco_s[m], wco_in[m], ("wco", m))
        cast(wmo_s[m], wmo_in[m], ("wmo", m))
    for k in range(8):
        cast(woa_s[k * 128:(k + 1) * 128, :], woa_in[k * 128:(k + 1) * 128, :], ("woa", k))
    cast_ffn(1)

    use(0)
    rms_to_hT(tiles[0], 0)
    for k, ti in enumerate(tiles):
        use(k)
        ffn(ti, 0, pre=True)
        if k + 1 < len(tiles):
            load_x(k + 1)
        in_all(ti)
        if ti.kind == "spec":
            P.pool(lambda e: e.tensor_copy(out=kT_meta[:], in_=kTn[0:96, :, 0:NMETA]), reads=R_kT, writes=[R_kTm])
            P.pool(lambda e: e.tensor_copy(out=v_meta[:], in_=vn[0:NMETA, :, 0, :]), reads=[R_vn], writes=[R_vm])
            for sbi in range(2):
                for r in range(2):
                    P.dma("pool", ch_io, nconv_s[sbi, r, :].rearrange("(c p) -> p c", p=128), ncs[:, :, sbi, r],
                          reads=[R_ncs], allow_slow_non_contiguous=True)
            sample_attention()
        else:
            b, i = ti.b, ti.i
            if i < NT - 1:
                P.dma("pool", ch_io, kS[b, i].rearrange("h p n -> p h n"), kTn[0:96, :, :], reads=R_kT,
                      writes=[sres(("kS", b, i))])
                P.dma("pool", ch_io, vS[b, i].rearrange("h p n -> p h n"),
                      vn[:].rearrange("p h s c -> p h (s c)"), reads=[R_vn], writes=[sres(("vS", b, i))])
            attention(ti)
        merge(ti)

        def hook(k=k):
            if k + 1 < len(tiles):
                use(k + 1)
                rms_to_hT(tiles[k + 1], 0)
                use(k)

        ffn(ti, 1, mid_hook=hook)
        xt_k = xbufs[k % 2][0]
        if ti.kind == "spec":
            for sbi in range(2):
                r0 = NMETA + DEC * sbi
                P.dma("pool", ch_io, y_s[sbi], xt_k[r0:r0 + DEC, 0, :], reads=axt(k))
        else:
            P.dma("pool", ch_io, y_p[ti.b, ti.i * T:(ti.i + 1) * T, :].rearrange("(s p) d -> p s d", p=128), xt_k[:],
                  reads=axt(k))
            if ti.i == NT - 1:
                for r in range(2):
                    P.dma("pool", ch_io, nconv_p[ti.b, r, :].rearrange("(c p) -> p c", p=128), halo[:, :, r],
                          reads=[R_halo], allow_slow_non_contiguous=True)

    P.emit()
    st.close()
    return nc


def _unit_image(w, c0, ncols=128):
    blk = w[:, c0:c0 + ncols].reshape(8, 128, ncols)
    return np.ascontiguousarray(blk.transpose(1, 0, 2).reshape(128, 8 * ncols))


def _rope_tables():
    half = 16
    inv_freq = (np.float32(10000.0) ** (-np.arange(half, dtype=np.float32) / np.float32(half))).astype(np.float32)
    pos = np.concatenate([np.arange(NMETA), NMETA + PAST + np.arange(DEC), NMETA + PAST + np.arange(DEC),
                          NMETA + np.arange(SEQ)]).astype(np.float32)
    ang = pos[:, None] * inv_freq[None, :]
    cos = np.cos(ang).astype(np.float32)
    sin = np.sin(ang).astype(np.float32)
    cs_tm = np.concatenate([cos, cos, sin, sin], axis=1)
    cs_fm = np.zeros((128, cos.shape[0]), np.float32)
    cs_fm[64:96] = np.concatenate([cos, cos], 1).T
    cs_fm[96:128] = np.concatenate([sin, sin], 1).T
    return np.ascontiguousarray(cs_fm), np.ascontiguousarray(cs_tm)


_NC_CACHE = {}


def kernel(x_prompt, x_sample, cache_conv, cache_kv_latent, cache_k_rope, meta_tokens,
           ffn1_norm, ffn1_w_gate, ffn1_w_up, ffn1_w_down, mix_norm, w_in_all, conv_w,
           w_conv_out, q_a_norm, w_uq, kv_a_norm, w_ukv, q_norm, k_norm, w_mla_out,
           w_out_all, ffn2_norm, ffn2_w_gate, ffn2_w_up, ffn2_w_down):
    f = lambda a: np.asarray(a, dtype=np.float32)
    x_prompt, x_sample = f(x_prompt), f(x_sample)
    cache_conv, cache_kv_latent, cache_k_rope = f(cache_conv)[0], f(cache_kv_latent)[0], f(cache_k_rope)[0]
    win = f(w_in_all)[0]
    shared = {}
    for nm, wg, wu, wd in (("1", ffn1_w_gate, ffn1_w_up, ffn1_w_down), ("2", ffn2_w_gate, ffn2_w_up, ffn2_w_down)):
        wg, wu, wd = f(wg)[0], f(wu)[0], f(wd)[0]
        shared["wg" + nm] = np.stack([_unit_image(wg, j * 128) for j in range(NJ)])
        shared["wu" + nm] = np.stack([_unit_image(wu, j * 128) for j in range(NJ)])
        shared["wd" + nm] = np.ascontiguousarray(wd)
    offs = {"b": 0, "c": 1024, "v": 2048, "q": 3072, "kv": 3456, "gc": 3616, "gm": 4640}
    units = [offs["c"] + 128 * j for j in range(8)] + [offs["v"] + 128 * j for j in range(8)] + \
            [offs["b"] + 128 * j for j in range(8)] + [offs["q"] + 128 * j for j in range(3)] + \
            [offs["gc"] + 128 * j for j in range(8)] + [offs["gm"] + 128 * j for j in range(8)]
    shared["win_u"] = np.stack([_unit_image(win, c0) for c0 in units])
    shared["wkv"] = _unit_image(win, offs["kv"], 160)
    shared["wco_u"] = np.stack([_unit_image(f(w_conv_out)[0], m * 128) for m in range(8)])
    shared["wmo_u"] = np.stack([_unit_image(f(w_mla_out)[0], m * 128) for m in range(8)])
    shared["woa"] = np.ascontiguousarray(f(w_out_all)[0])
    wuq = f(w_uq)[0].reshape(3, 128, NH, 96)
    wq_c = np.concatenate([wuq, wuq[..., 80:96], wuq[..., 64:80]], axis=-1)
    shared["wq_p"] = np.ascontiguousarray(wq_c.transpose(1, 0, 2, 3).reshape(128, -1))
    wukv = f(w_ukv)[0].reshape(128, NH, 128)
    wka = np.concatenate([wukv[:, :, 0:64], np.zeros((128, NH, 64), np.float32)], axis=-1)
    shared["wka"] = np.ascontiguousarray(wka.reshape(128, -1))
    shared["wuv"] = np.ascontiguousarray(wukv[:, :, 64:128].reshape(128, -1))
    bsel = np.zeros((32, 128), np.float32)
    bsel[np.arange(32), 64 + np.arange(32)] = 1.0
    shared["bsel"] = bsel
    shared["ident"] = np.eye(128, dtype=np.float32)
    gT = np.concatenate([f(g)[0].reshape(8, 128).T for g in (ffn1_norm, mix_norm, ffn2_norm)], axis=1)
    shared["gT"] = np.ascontiguousarray(gT)
    shared["gqaT"] = np.ascontiguousarray(f(q_a_norm)[0].reshape(3, 128).T)
    shared["gkv_bc"] = np.ascontiguousarray(np.broadcast_to(f(kv_a_norm)[0][None, :], (128, 128)))
    perm = np.concatenate([np.arange(64, 96), np.arange(0, 64)])
    shared["gq_p"] = np.ascontiguousarray(f(q_norm)[0][:, None])
    shared["gk_p"] = np.ascontiguousarray(f(k_norm)[0][:, None])
    cw = f(conv_w)[0]
    shared["cwT"] = np.ascontiguousarray(cw.reshape(3, 8, 128).transpose(2, 1, 0).reshape(128, 24))
    cs_fm, cs_tm = _rope_tables()
    shared["cs_fm"] = cs_fm
    shared["cs_tm"] = cs_tm
    shared["meta"] = np.ascontiguousarray(f(meta_tokens))

    if "nc" not in _NC_CACHE:
        _NC_CACHE["nc"] = build_program()
    nc = _NC_CACHE["nc"]
    in_maps = []
    for c in range(NCORES):
        m = dict(shared)
        m["x_p"] = np.ascontiguousarray(x_prompt[2 * c:2 * c + 2])
        m["x_s"] = np.ascontiguousarray(x_sample[2 * c:2 * c + 2])
        m["c_conv"] = np.ascontiguousarray(cache_conv[2 * c:2 * c + 2])
        m["c_kv"] = np.ascontiguousarray(cache_kv_latent[2 * c:2 * c + 2])
        m["c_kr"] = np.ascontiguousarray(cache_k_rope[2 * c:2 * c + 2])
        in_maps.append(m)
    res = run_bass_kernel_spmd(nc, in_maps, core_ids=list(range(NCORES)))
    R = res.results
    cat = lambda k: np.concatenate([np.asarray(r[k], dtype=np.float32) for r in R], axis=0)
    return (cat("y_p"), cat("y_s"), cat("nconv_p")[None], cat("nkv_p")[None], cat("nkr_p")[None],
            cat("nconv_s")[None], cat("nkv_s")[None], cat("nkr_s")[None])
```
